# Optimizing a Trainium2 kernel written in Bass

```python
import math
import jax, jax.numpy as jnp
from jax import lax
import numpy as np

D_MODEL = 2048
BATCH = 4
SEQ = 2048
DEPTH = 4
DEC_BATCH = 128
DEC_SEQ = 1
PAST_LEN = 16384
PAGE_SIZE = 128

N_MIXERS = 2
EXPAND = 2
D_INNER = EXPAND * D_MODEL
CONV_WIDTH = 3
GROUP_SIZE = 16
N_GROUPS = D_INNER // GROUP_SIZE
STATE_DIM = 64
SCAN_CHUNK = 128
N_CONV_LAYERS = (DEPTH + 1) // N_MIXERS
N_SSM_LAYERS = DEPTH // N_MIXERS
EPS = 1e-6

kernel_name = "hybrid_shortconv_s5_decode_step"


def rmsnorm(x, g):
    xf = x.astype(jnp.float32)
    xf = xf * lax.rsqrt(jnp.mean(xf * xf, axis=-1, keepdims=True) + EPS)
    return (xf * g.astype(jnp.float32)).astype(x.dtype)


def short_conv_mixer(h, state, w_in, conv_w, w_out):
    L = h.shape[1]
    proj = h @ w_in
    b_gate, c_gate, v, z = jnp.split(proj, 4, axis=-1)
    u = c_gate * v
    padded = jnp.concatenate([state.astype(u.dtype), u], axis=1)
    conv = sum(conv_w[k] * padded[:, k:k + L] for k in range(CONV_WIDTH))
    y = b_gate * conv * jax.nn.silu(z)
    new_state = padded[:, -(CONV_WIDTH - 1):]
    return y @ w_out, new_state


def _scan_combine(e1, e2):
    a1, b1 = e1
    a2, b2 = e2
    return (a2 * a1, a2 * b1 + b2)


def s5_mixer(h, state_re, state_im, w_in, a_re, a_im, log_dt, b_re, b_im, c_re, c_im,
             d_skip, w_glu, b_glu, w_out):
    Bsz, L, _ = h.shape
    proj = h @ w_in
    u, z = jnp.split(proj, 2, axis=-1)
    uf = u.astype(jnp.float32)
    f32 = jnp.float32
    lam = lax.complex(a_re.astype(f32), a_im.astype(f32))
    dt = jnp.exp(log_dt.astype(f32))[:, None]
    lam_bar = jnp.exp(lam * dt)
    b_c = lax.complex(b_re.astype(f32), b_im.astype(f32))
    b_bar = ((lam_bar - 1.0) / lam)[..., None] * b_c
    c_c = lax.complex(c_re.astype(f32), c_im.astype(f32))
    h0 = lax.complex(state_re.astype(f32), state_im.astype(f32))

    chunk = SCAN_CHUNK if (L % SCAN_CHUNK == 0) else L
    n_chunks = L // chunk
    u_chunks = uf.reshape(Bsz, n_chunks, chunk, N_GROUPS, GROUP_SIZE).transpose(1, 2, 0, 3, 4)
    a_elems = jnp.broadcast_to(lam_bar, (chunk, 1, N_GROUPS, STATE_DIM))

    def step(carry, u_c):
        bu = jnp.einsum('tbgh,gph->tbgp', u_c.astype(jnp.complex64), b_bar)
        bu = bu.at[0].add(lam_bar * carry)
        _, states = lax.associative_scan(_scan_combine, (a_elems, bu), axis=0)
        y_c = jnp.real(jnp.einsum('ghp,tbgp->tbgh', c_c, states))
        return states[-1], y_c

    h_last, ys = lax.scan(step, h0, u_chunks)
    y = ys.transpose(2, 0, 1, 3, 4).reshape(Bsz, L, D_INNER)
    y = y + d_skip.astype(f32) * uf
    y = jax.nn.gelu(y)
    y = y * jax.nn.sigmoid(y @ w_glu.astype(f32) + b_glu.astype(f32))
    out = (y.astype(h.dtype) * jax.nn.silu(z)) @ w_out
    return out, jnp.real(h_last), jnp.imag(h_last)


def setup_inputs(seed: int = 0) -> dict:
    key = jax.random.key(seed)
    ks = jax.random.split(key, 24)
    f32 = jnp.float32
    nc, ns = N_CONV_LAYERS, N_SSM_LAYERS
    x_prompt = jax.random.normal(ks[0], (BATCH, SEQ, D_MODEL), f32)
    x_sample = jax.random.normal(ks[1], (DEC_BATCH, DEC_SEQ, D_MODEL), f32)
    state_conv = jax.random.normal(ks[2], (nc, DEC_BATCH, CONV_WIDTH - 1, D_INNER), f32)
    state_ssm_re = jax.random.normal(ks[3], (ns, DEC_BATCH, N_GROUPS, STATE_DIM), f32)
    state_ssm_im = jax.random.normal(ks[4], (ns, DEC_BATCH, N_GROUPS, STATE_DIM), f32)

    conv_norm = 1.0 + 0.02 * jax.random.normal(ks[5], (nc, D_MODEL), f32)
    conv_w_in = jax.random.normal(ks[6], (nc, D_MODEL, 4 * D_INNER), f32) * D_MODEL ** -0.5
    conv_w = jax.random.normal(ks[7], (nc, CONV_WIDTH, D_INNER), f32) * CONV_WIDTH ** -0.5
    conv_w_out = jax.random.normal(ks[8], (nc, D_INNER, D_MODEL), f32) * D_INNER ** -0.5

    ssm_norm = 1.0 + 0.02 * jax.random.normal(ks[9], (ns, D_MODEL), f32)
    ssm_w_in = jax.random.normal(ks[10], (ns, D_MODEL, 2 * D_INNER), f32) * D_MODEL ** -0.5
    ssm_a_re = -0.5 + 0.01 * jax.random.normal(ks[11], (ns, N_GROUPS, STATE_DIM), f32)
    n_idx = jnp.arange(STATE_DIM, dtype=f32)
    ssm_a_im = math.pi * n_idx + 0.01 * jax.random.normal(ks[12], (ns, N_GROUPS, STATE_DIM), f32)
    ssm_log_dt = jax.random.uniform(ks[13], (ns, N_GROUPS), f32, math.log(1e-3), math.log(1e-1))
    ssm_b_re = jax.random.normal(ks[14], (ns, N_GROUPS, STATE_DIM, GROUP_SIZE), f32) * (2 * GROUP_SIZE) ** -0.5
    ssm_b_im = jax.random.normal(ks[15], (ns, N_GROUPS, STATE_DIM, GROUP_SIZE), f32) * (2 * GROUP_SIZE) ** -0.5
    ssm_c_re = jax.random.normal(ks[16], (ns, N_GROUPS, GROUP_SIZE, STATE_DIM), f32) * (2 * STATE_DIM) ** -0.5
    ssm_c_im = jax.random.normal(ks[17], (ns, N_GROUPS, GROUP_SIZE, STATE_DIM), f32) * (2 * STATE_DIM) ** -0.5
    ssm_d = jax.random.normal(ks[18], (ns, D_INNER), f32) * 0.5
    ssm_w_glu = jax.random.normal(ks[19], (ns, D_INNER, D_INNER), f32) * D_INNER ** -0.5
    ssm_b_glu = 0.01 * jax.random.normal(ks[20], (ns, D_INNER), f32)
    ssm_w_out = jax.random.normal(ks[21], (ns, D_INNER, D_MODEL), f32) * D_INNER ** -0.5
    final_norm = 1.0 + 0.02 * jax.random.normal(ks[22], (D_MODEL,), f32)
    return {
        "x_prompt": x_prompt, "x_sample": x_sample,
        "state_conv": state_conv, "state_ssm_re": state_ssm_re, "state_ssm_im": state_ssm_im,
        "conv_norm": conv_norm, "conv_w_in": conv_w_in, "conv_w": conv_w, "conv_w_out": conv_w_out,
        "ssm_norm": ssm_norm, "ssm_w_in": ssm_w_in, "ssm_a_re": ssm_a_re, "ssm_a_im": ssm_a_im,
        "ssm_log_dt": ssm_log_dt, "ssm_b_re": ssm_b_re, "ssm_b_im": ssm_b_im,
        "ssm_c_re": ssm_c_re, "ssm_c_im": ssm_c_im, "ssm_d": ssm_d,
        "ssm_w_glu": ssm_w_glu, "ssm_b_glu": ssm_b_glu, "ssm_w_out": ssm_w_out,
        "final_norm": final_norm,
    }


def reference(x_prompt, x_sample, state_conv, state_ssm_re, state_ssm_im,
              conv_norm, conv_w_in, conv_w, conv_w_out,
              ssm_norm, ssm_w_in, ssm_a_re, ssm_a_im, ssm_log_dt, ssm_b_re, ssm_b_im,
              ssm_c_re, ssm_c_im, ssm_d, ssm_w_glu, ssm_b_glu, ssm_w_out, final_norm):
    xp, xs = x_prompt, x_sample
    conv_p, conv_s = [], []
    re_p, im_p, re_s, im_s = [], [], [], []
    for i in range(DEPTH):
        j = i // N_MIXERS
        if i % N_MIXERS == 0:
            zero_buf = jnp.zeros((xp.shape[0], CONV_WIDTH - 1, D_INNER), xp.dtype)
            op, sp = short_conv_mixer(rmsnorm(xp, conv_norm[j]), zero_buf,
                                      conv_w_in[j], conv_w[j], conv_w_out[j])
            os_, ss = short_conv_mixer(rmsnorm(xs, conv_norm[j]), state_conv[j],
                                       conv_w_in[j], conv_w[j], conv_w_out[j])
            xp = xp + op
            xs = xs + os_
            conv_p.append(sp)
            conv_s.append(ss)
        else:
            params = (ssm_w_in[j], ssm_a_re[j], ssm_a_im[j], ssm_log_dt[j], ssm_b_re[j], ssm_b_im[j],
                      ssm_c_re[j], ssm_c_im[j], ssm_d[j], ssm_w_glu[j], ssm_b_glu[j], ssm_w_out[j])
            zero_h = jnp.zeros((xp.shape[0], N_GROUPS, STATE_DIM), jnp.float32)
            op, hr_p, hi_p = s5_mixer(rmsnorm(xp, ssm_norm[j]), zero_h, zero_h, *params)
            os_, hr_s, hi_s = s5_mixer(rmsnorm(xs, ssm_norm[j]), state_ssm_re[j], state_ssm_im[j], *params)
            xp = xp + op
            xs = xs + os_
            re_p.append(hr_p)
            im_p.append(hi_p)
            re_s.append(hr_s)
            im_s.append(hi_s)
    y_prompt = rmsnorm(xp, final_norm)
    y_sample = rmsnorm(xs, final_norm)
    new_conv_prompt = jnp.stack(conv_p)
    new_conv_sample = jnp.stack(conv_s)
    new_ssm_re_prompt = jnp.stack(re_p)
    new_ssm_im_prompt = jnp.stack(im_p)
    new_ssm_re_sample = jnp.stack(re_s)
    new_ssm_im_sample = jnp.stack(im_s)
    return (y_prompt, y_sample, new_conv_prompt, new_conv_sample,
            new_ssm_re_prompt, new_ssm_im_prompt, new_ssm_re_sample, new_ssm_im_sample)
```

```python
import contextlib
import numpy as np
import concourse.bass as bass
import concourse.mybir as mybir
from concourse.bass_utils import run_bass_kernel_spmd

F32 = mybir.dt.float32
BF16 = mybir.dt.bfloat16
AF = mybir.ActivationFunctionType
ALU = mybir.AluOpType

D = 2048
DI = 4096
KD = 16
QI = 32
NT = 512
NS = 16
NX = NT + NS
NCH = 64
G = 256
P = 64
H = 16
SEQ = 2048
NSEG = 4
EPS = 1e-6
NCORES = 8
GB = 16
NBATCH = 128 // GB
SAME_ENGINE_SYNC = True
SSM_STOP = 99


class PT:
    def __init__(self, t, name):
        self.t = t
        self.name = name

    def __getitem__(self, k):
        return self.t[k]


class Prog:
    ENG = ("pe", "act", "dve", "pool", "sp")

    def __init__(self, nc, es):
        self.nc = nc
        self.es = es
        self.streams = {e: [] for e in self.ENG}
        self.esem = {}
        self.ecount = {e: 0 for e in self.ENG}
        self.dsems = {}
        self.waited = {e: {} for e in self.ENG}
        self.lastw = {}
        self.readers = {}
        self.sems = {}
        self.n_ops = 0

    def _sem(self, key):
        if key not in self.sems:
            self.sems[key] = self.es.enter_context(self.nc.semaphore("s_%s_%s" % key))
        return self.sems[key]

    def _wait(self, eng, tok):
        key, val = tok
        if key == ("e", eng) and not (SAME_ENGINE_SYNC and eng != "pe"):
            return
        if self.waited[eng].get(key, 0) >= val:
            return
        self.waited[eng][key] = val
        sem = self._sem(key)
        self.streams[eng].append(lambda e, sem=sem, val=val: e.wait_ge(sem, val))

    def _deps(self, eng, reads, writes):
        for r in reads:
            t = self.lastw.get(r)
            if t is not None:
                self._wait(eng, t)
        for w in writes:
            t = self.lastw.get(w)
            if t is not None:
                self._wait(eng, t)
            for t in list(self.readers.get(w, {}).items()):
                self._wait(eng, t)

    def _commit(self, tok, reads, writes):
        for w in writes:
            self.lastw[w] = tok
            self.readers[w] = {}
        for r in reads:
            if r in writes:
                continue
            d = self.readers.setdefault(r, {})
            if d.get(tok[0], 0) < tok[1]:
                d[tok[0]] = tok[1]

    PSUM_RES = frozenset(["mm0", "mm1", "mm2", "mm3", "sm0", "sm1", "tp0", "tp1"])

    def op(self, eng, fn, reads=(), writes=(), inc=True):
        self.n_ops += 1
        xr = [r for r in reads if r in self.PSUM_RES and r not in writes]
        if xr:
            writes = list(writes) + xr
        self._deps(eng, reads, writes)
        key = ("e", eng)
        if inc:
            self.ecount[eng] += 1
            val = self.ecount[eng]
            sem = self._sem(key)
            self.streams[eng].append(lambda e, fn=fn, sem=sem: fn(e).then_inc(sem, 1))
        else:
            val = self.ecount[eng] + 1
            self.streams[eng].append(lambda e, fn=fn: fn(e))
        self._commit((key, val), reads, writes)

    def dma(self, q, out, in_, reads=(), writes=(), dsem="d"):
        self.n_ops += 1
        self._deps(q, reads, writes)
        key = ("d", dsem)
        self.dsems[key] = self.dsems.get(key, 0) + 16
        val = self.dsems[key]
        sem = self._sem(key)
        self.streams[q].append(lambda e, out=out, in_=in_, sem=sem: e.dma_start(out=out, in_=in_).then_inc(sem, 16))
        self._commit((key, val), reads, writes)

    def seal(self, dsem, resources):
        key = ("d", dsem)
        for r in resources:
            self.lastw[r] = (key, self.dsems[key])

    def final_wait(self, eng):
        for key, val in self.dsems.items():
            self._wait(eng, (key, val))
        for e2 in self.ENG:
            if e2 != eng and self.ecount[e2] > 0:
                self._wait(eng, (("e", e2), self.ecount[e2]))

    def emit(self):
        nc = self.nc
        with nc.Block() as block:
            @block.tensor
            def _(e):
                for f in self.streams["pe"]:
                    f(e)

            @block.scalar
            def _(e):
                for f in self.streams["act"]:
                    f(e)

            @block.vector
            def _(e):
                for f in self.streams["dve"]:
                    f(e)

            @block.gpsimd
            def _(e):
                for f in self.streams["pool"]:
                    f(e)

            @block.sync
            def _(e):
                for f in self.streams["sp"]:
                    f(e)


def build_program(nseg=NSEG, nlayers=4, dump_x=False, skip_ssm=False):
    nc = bass.Bass("TRN2", target_bir_lowering=False)
    es = contextlib.ExitStack()

    def din(name, shape):
        return nc.dram_tensor(name, list(shape), F32, kind="ExternalInput").ap()

    def dout(name, shape):
        return nc.dram_tensor(name, list(shape), F32, kind="ExternalOutput").ap()

    xp_d = din("xp", [SEQ, D])
    xs_d = din("xs", [NS, D])
    sconv_d = din("sconv", [2, NS, 2, DI])
    sre_d = din("sre", [2, NS, G, P])
    sim_d = din("sim", [2, NS, G, P])
    conv_norm_d = din("conv_norm", [2, D])
    conv_w_in_d = din("conv_w_in", [2, D, 4 * DI])
    conv_w_d = din("conv_w", [2, 3, DI])
    conv_w_out_d = din("conv_w_out", [2, DI, D])
    ssm_norm_d = din("ssm_norm", [2, D])
    ssm_w_in_d = din("ssm_w_in", [2, D, 2 * DI])
    a_re_d = din("ssm_a_re", [2, G, P])
    a_im_d = din("ssm_a_im", [2, G, P])
    ldt_d = din("ssm_log_dt", [2, G])
    b_re_d = din("ssm_b_re", [2, G, P, H])
    b_im_d = din("ssm_b_im", [2, G, P, H])
    c_re_d = din("ssm_c_re", [2, G, H, P])
    c_im_d = din("ssm_c_im", [2, G, H, P])
    ssm_d_d = din("ssm_d", [2, DI])
    w_glu_d = din("ssm_w_glu", [2, DI, DI])
    b_glu_d = din("ssm_b_glu", [2, DI])
    ssm_w_out_d = din("ssm_w_out", [2, DI, D])
    fnorm_d = din("final_norm", [D])

    yp_o = dout("yp", [SEQ, D])
    ys_o = dout("ys", [NS, D])
    ncp_o = dout("ncp", [2, 2, DI])
    ncs_o = dout("ncs", [2, NS, 2, DI])
    nrp_o = dout("nrp", [2, G, P])
    nip_o = dout("nip", [2, G, P])
    nrs_o = dout("nrs", [2, NS, G, P])
    nis_o = dout("nis", [2, NS, G, P])
    dbg_o = dout("dbg", [5, 128, KD, NX]) if dump_x else None
    dbgA_o = nc.dram_tensor("dbgA", [2, 128, QI, NS], BF16, kind="ExternalOutput").ap() if dump_x else None
    dbgS_o = dout("dbgS", [2, 128, 4, NS]) if dump_x else None

    with es:
        pr = Prog(nc, es)

        def sb(name, shape, dt=F32):
            return es.enter_context(nc.sbuf_tensor(name, list(shape), dt))

        def ps(name, shape, dt=F32):
            return es.enter_context(nc.psum_tensor(name, list(shape), dt))

        x_t = sb("x", [128, KD, NX])
        hT_t = sb("hT", [128, KD, NX], BF16)
        A_t = sb("A", [128, QI, NX], BF16)
        B_t = sb("B", [128, QI * NX], BF16)
        NWS = 4
        wbuf = [sb("w%d" % i, [128, KD, 128], BF16) for i in range(NWS)]
        ident = sb("ident", [128, 128])
        identb = sb("identb", [128, 128], BF16)
        onesb = sb("onesb", [128, 128], BF16)
        maskT = sb("maskT", [128, 8, 16])
        gains = sb("gains", [128, 5, KD])
        convw = sb("convw", [128, 2, 3, QI])
        dnat = sb("dnat", [128, 2, QI])
        bglu = sb("bglu", [128, 2, QI])
        dgm = sb("dgm", [128, 2, G])
        Xfin = sb("Xfin", [128, 2, 2, 128])
        ucar = sb("ucar", [128, 2, QI, 2])
        rstd = sb("rstd", [128, NX])
        scr = sb("scr", [128, 4, 516])
        scrf0 = scr[:].rearrange("p a b -> p (a b)")
        rtmp = scrf0[:, 0:NX]
        smp = sb("smp", [128, 3, 576])
        smpf = smp[:].rearrange("p a b -> p (a b)")
        xin_v = hT_t[:].rearrange("p a b -> p (a b)")[:, 0:2 * D].bitcast(F32)
        yfm_v = A_t[:].rearrange("p a b -> p (a b)")[:, 0:2 * KD * 128].bitcast(F32).rearrange("p (k n) -> p k n", k=KD)
        sq_v = B_t[:, 0:KD * NX].rearrange("p (k n) -> p k n", k=KD)
        uh_v = B_t[:, 0:G * NCH].rearrange("p (g c) -> p g c", c=NCH)
        y3_v = B_t[:].rearrange("p (q n) -> p q n", q=QI)
        uhs = sb("uhs", [128, G, NS], BF16)

        mm = [PT(ps("mm%d" % i, [128, 512]), "mm%d" % i) for i in range(4)]
        sm = [PT(ps("sm%d" % i, [128, 512]), "sm%d" % i) for i in range(2)]
        tp = [PT(ps("tp%d" % i, [128, 512]), "tp%d" % i) for i in range(2)]

        st = {"w": 0, "mm": 0, "sm": 0, "tp": 0, "TP": 0}

        def nxt(kind, n):
            i = st[kind]
            st[kind] = (i + 1) % n
            return i

        pr.op("pool", lambda e: e.memset(ident[:], 1.0), writes=["ident"])
        pr.op("pool", lambda e: e.affine_select(out=ident[:], in_=ident[:], pattern=[[-1, 128]],
                                                compare_op=ALU.is_equal, fill=0.0, base=0, channel_multiplier=1),
              reads=["ident"], writes=["ident"])
        pr.op("pool", lambda e: e.memset(onesb[:], 1.0), writes=["onesb"])
        ones_f = sb("ones_f", [1, 128])
        pr.op("pool", lambda e: e.memset(ones_f[:], 1.0), writes=["ones_f"])
        pr.op("pool", lambda e: e.memset(maskT[:], 1.0), writes=["maskT"])
        pr.op("pool", lambda e: e.affine_select(out=maskT[:], in_=maskT[:], pattern=[[16, 8], [0, 16]],
                                                compare_op=ALU.is_ge, fill=0.0, base=15, channel_multiplier=-1),
              reads=["maskT"], writes=["maskT"])
        pr.op("pool", lambda e: e.memset(ucar[:], 0.0), writes=["ucar"])
        pr.op("pool", lambda e: e.memset(Xfin[:], 0.0), writes=["Xfin"])
        pr.op("pool", lambda e: e.memset(uhs[:], 0.0), writes=["uhs"])
        pr.op("dve", lambda e: e.tensor_copy(identb[:], ident[:]), reads=["ident"], writes=["identb"])

        pstage = scrf0
        plist = []
        for i, src in enumerate([conv_norm_d[0], ssm_norm_d[0], conv_norm_d[1], ssm_norm_d[1], fnorm_d]):
            plist.append((src, gains[:, i, :], "gains"))
        for l in range(2):
            for k in range(3):
                plist.append((conv_w_d[l, k], convw[:, l, k, :], "convw"))
            plist.append((ssm_d_d[l], dnat[:, l, :], "dnat"))
            plist.append((b_glu_d[l], bglu[:, l, :], "bglu"))
        for i, (src, dst, res) in enumerate(plist):
            n = src.shape[0] // 128
            pr.dma("sp", pstage[0:n, i * 128:(i + 1) * 128], src.rearrange("(k p) -> k p", p=128), writes=["scr0", "scr1", "scr2", "scr3"], dsem="par")
        pr.seal("par", ["scr0", "scr1", "scr2", "scr3"])
        for i, (src, dst, res) in enumerate(plist):
            n = src.shape[0] // 128
            tpt = tp[nxt("tp", 2)]
            pr.op("pe", lambda e, tpt=tpt, i=i, n=n: e.transpose(tpt[:, 0:n], pstage[0:n, i * 128:(i + 1) * 128], ident[0:n, 0:n]),
                  reads=["scr0", "scr1", "scr2", "scr3", "ident"], writes=[tpt.name])
            pr.op("dve", lambda e, tpt=tpt, dst=dst, n=n: e.tensor_copy(dst, tpt[:, 0:n]), reads=[tpt.name], writes=[res])

        Esel = scr[:].rearrange("p a b -> p (a b)")[:, 0:1024].rearrange("p (a b) -> p a b", a=8)
        pr.op("pool", lambda e: e.memset(Esel, 1.0), writes=["scr0", "scr1"])
        for gl in range(8):
            pr.op("pool", lambda e, gl=gl: e.affine_select(out=Esel[:, gl, :].rearrange("p (j h) -> p j h", j=8),
                                                           in_=Esel[:, gl, :].rearrange("p (j h) -> p j h", j=8),
                                                           pattern=[[0, 8], [-1, 16]], compare_op=ALU.is_equal, fill=0.0,
                                                           base=-16 * gl, channel_multiplier=1),
                  reads=["scr0", "scr1"], writes=["scr0", "scr1"])
        for l in range(2):
            pst = sm[nxt("sm", 2)]
            pv = pst[:, 0:256].rearrange("p (q g) -> p g q", g=8)
            for gl in range(8):
                pr.op("pe", lambda e, gl=gl, pv=pv, l=l: e.matmul(pv[:, gl, :], Esel[:, gl, :], dnat[:, l, :], start=True, stop=True),
                      reads=["scr0", "scr1", "dnat"], writes=[pst.name])
            pr.op("dve", lambda e, pst=pst, l=l: e.tensor_copy(dgm[:, l, :], pst[:, 0:256]), reads=[pst.name], writes=["dgm"])

        def colgroups(seg):
            return [(0, NT)] + ([(NT, NS)] if seg == 0 else [])

        def ncols(seg):
            return NX if seg == 0 else NT

        def load_weights(view):
            KC = view.shape[0] // 128
            parts = []
            for k0 in range(0, KC, KD):
                s = nxt("w", NWS)
                pr.dma("pool", wbuf[s][:, :, :], view[k0 * 128:(k0 + KD) * 128, :].rearrange("(k p) n -> p k n", p=128),
                       writes=[("w", s)], dsem="w%d" % s)
                parts.append((s, k0))
            return parts, KC

        def mm_chunk(view, rhs_fn, rhs_res, seg, big_bank, small_tile, small_off):
            parts, KC = load_weights(view)
            outs = []
            for (c0, n) in colgroups(seg):
                if n == NT:
                    o = mm[big_bank][:, 0:NT]
                    res = mm[big_bank].name
                else:
                    o = small_tile[:, small_off:small_off + NS]
                    res = small_tile.name
                for (s, k0) in parts:
                    for kk in range(KD):
                        k = k0 + kk
                        pr.op("pe", lambda e, o=o, s=s, kk=kk, k=k, c0=c0, n=n: e.matmul(o, wbuf[s][:, kk, :], rhs_fn(k, c0, n),
                                                                                        start=(k == 0), stop=(k == KC - 1)),
                              reads=[("w", s)] + rhs_res, writes=[res], inc=(k == KC - 1))
                outs.append((o, res, c0, n))
            return outs

        def rmsnorm(gi, seg, jmajor):
            rms_stats(seg)
            for k in range(KD):
                if jmajor:
                    o = hT_t[:, k, 0:NT].rearrange("p (j c) -> p c j", j=8)
                    i0 = x_t[:, k, 0:NT].rearrange("p (c j) -> p c j", j=8)
                    i1 = rstd[:, 0:NT].rearrange("p (c j) -> p c j", j=8)
                else:
                    o, i0, i1 = hT_t[:, k, 0:NT], x_t[:, k, 0:NT], rstd[:, 0:NT]
                pr.op("dve", lambda e, o=o, i0=i0, i1=i1, k=k: e.scalar_tensor_tensor(o, i0, gains[:, gi, k:k + 1], i1, ALU.mult, ALU.mult),
                      reads=["x", "rstd", "gains"], writes=["hT"])
                if seg == 0:
                    pr.op("dve", lambda e, k=k: e.scalar_tensor_tensor(hT_t[:, k, NT:NX], x_t[:, k, NT:NX], gains[:, gi, k:k + 1],
                                                                       rstd[:, NT:NX], ALU.mult, ALU.mult),
                          reads=["x", "rstd", "gains"], writes=["hT"])

        def load_x(seg):
            for t in range(NT // 128):
                r0 = seg * NT + t * 128
                pr.dma("sp", xin_v, xp_d[r0:r0 + 128, :], writes=["hT"], dsem="xin")
                for kb in range(4):
                    tpt = tp[nxt("tp", 2)]
                    for kk in range(4):
                        k = kb * 4 + kk
                        pr.op("pe", lambda e, tpt=tpt, kk=kk, k=k: e.transpose(tpt[:, kk * 128:(kk + 1) * 128], xin_v[:, k * 128:(k + 1) * 128], ident[:]),
                              reads=["hT", "ident"], writes=[tpt.name], inc=(kk == 3))
                    pr.op("dve", lambda e, tpt=tpt, kb=kb, t=t: e.tensor_copy(x_t[:, kb * 4:kb * 4 + 4, t * 128:(t + 1) * 128],
                                                                              tpt[:, :].rearrange("p (a b) -> p a b", a=4)),
                          reads=[tpt.name], writes=["x"])
            if seg == 0:
                pr.dma("sp", xin_v[0:NS, :], xs_d[:, :], writes=["hT"], dsem="xin")
                tpt = tp[nxt("tp", 2)]
                for k in range(KD):
                    pr.op("pe", lambda e, tpt=tpt, k=k: e.transpose(tpt[:, k * NS:(k + 1) * NS], xin_v[0:NS, k * 128:(k + 1) * 128], ident[0:NS, 0:NS]),
                          reads=["hT", "ident"], writes=[tpt.name], inc=(k == KD - 1))
                pr.op("dve", lambda e, tpt=tpt: e.tensor_copy(x_t[:, :, NT:NX], tpt[:, 0:KD * NS].rearrange("p (a b) -> p a b", a=KD)),
                      reads=[tpt.name], writes=["x"])

        def final_out(seg):
            n = ncols(seg)
            rms_stats(seg)
            for t in range(NT // 128):
                for k in range(KD):
                    pr.op("dve", lambda e, k=k, t=t: e.scalar_tensor_tensor(yfm_v[:, k, :], x_t[:, k, t * 128:(t + 1) * 128], gains[:, 4, k:k + 1],
                                                                            rstd[:, t * 128:(t + 1) * 128], ALU.mult, ALU.mult),
                          reads=["x", "rstd", "gains"], writes=["A"])
                for kb in range(4):
                    tpt = tp[nxt("tp", 2)]
                    for kk in range(4):
                        k = kb * 4 + kk
                        pr.op("pe", lambda e, tpt=tpt, kk=kk, k=k: e.transpose(tpt[:, kk * 128:(kk + 1) * 128], yfm_v[:, k, :], ident[:]),
                              reads=["A", "ident"], writes=[tpt.name], inc=(kk == 3))
                    pr.op("act", lambda e, tpt=tpt, kb=kb: e.copy(xin_v[:, kb * 512:(kb + 1) * 512], tpt[:, :]),
                          reads=[tpt.name], writes=["hT"])
                r0 = seg * NT + t * 128
                pr.dma("sp", yp_o[r0:r0 + 128, :], xin_v, reads=["hT"], dsem="yx")
            if seg == 0:
                for k in range(KD):
                    pr.op("dve", lambda e, k=k: e.scalar_tensor_tensor(yfm_v[:, k, 0:NS], x_t[:, k, NT:NX], gains[:, 4, k:k + 1],
                                                                       rstd[:, NT:NX], ALU.mult, ALU.mult),
                          reads=["x", "rstd", "gains"], writes=["A"])
                for kb in range(4):
                    tpt = tp[nxt("tp", 2)]
                    for kk in range(4):
                        k = kb * 4 + kk
                        pr.op("pe", lambda e, tpt=tpt, kk=kk, k=k: e.transpose(tpt[0:NS, kk * 128:(kk + 1) * 128], yfm_v[:, k, 0:NS], ident[:]),
                              reads=["A", "ident"], writes=[tpt.name], inc=(kk == 3))
                    pr.op("act", lambda e, tpt=tpt, kb=kb: e.copy(xin_v[0:NS, kb * 512:(kb + 1) * 512], tpt[0:NS, :]),
                          reads=[tpt.name], writes=["hT"])
                pr.dma("sp", ys_o[:, :], xin_v[0:NS, :], reads=["hT"], dsem="yx")

        def rms_stats(seg):
            n = ncols(seg)
            for k in range(KD):
                pr.op("act", lambda e, k=k: e.activation(out=sq_v[:, k, 0:n], in_=x_t[:, k, 0:n], func=AF.Square),
                      reads=["x"], writes=["B"])
            for (c0, nn) in colgroups(seg):
                pst = sm[nxt("sm", 2)]
                for k in range(KD):
                    pr.op("pe", lambda e, k=k, c0=c0, nn=nn, pst=pst: e.matmul(pst[:, 0:nn], onesb[:], sq_v[:, k, c0:c0 + nn],
                                                                               start=(k == 0), stop=(k == KD - 1)),
                          reads=["onesb", "B"], writes=[pst.name], inc=(k == KD - 1))
                pr.op("act", lambda e, c0=c0, nn=nn, pst=pst: e.activation(out=rtmp[:, c0:c0 + nn], in_=pst[:, 0:nn], func=AF.Sqrt,
                                                                           bias=EPS, scale=1.0 / D),
                      reads=[pst.name], writes=["scr0", "scr1"])
                pr.op("dve", lambda e, c0=c0, nn=nn: e.reciprocal(rstd[:, c0:c0 + nn], rtmp[:, c0:c0 + nn]),
                      reads=["scr0", "scr1"], writes=["rstd"])

        def out_proj(w_d, src_v, src_res, seg, jmajor):
            for m in range(KD):
                bank = nxt("mm", 4)
                smt = sm[nxt("sm", 2)]
                outs = mm_chunk(w_d[:, m * 128:(m + 1) * 128], lambda k, c0, n: src_v[:, k, c0:c0 + n], [src_res], seg, bank, smt, 0)
                for (o, res, c0, n) in outs:
                    if n == NT and jmajor:
                        xo = x_t[:, m, 0:NT].rearrange("p (c j) -> p c j", j=8)
                        oo = o.rearrange("p (j c) -> p c j", j=8)
                    else:
                        xo, oo = x_t[:, m, c0:c0 + n], o
                    pr.op("dve", lambda e, xo=xo, oo=oo: e.tensor_tensor(xo, xo, oo, ALU.add), reads=[res, "x"], writes=["x"])

        def conv_layer(l, seg, last_seg):
            gi = 2 * l
            rmsnorm(gi, seg, jmajor=False)
            w_in = conv_w_in_d[l]
            ue = scr[:, 0, 0:NT + 2]
            ctmp = scr[:, 1, 0:NT]
            cv = scr[:, 2, 0:NT]
            sz = scr[:, 3, 0:NT]
            stc = smpf[:, 0:1024].rearrange("p (q r s) -> p q r s", q=QI, r=2)
            us = smp[:, 2, 0:QI * NS].rearrange("p (q s) -> p q s", q=QI)
            stmp = smp[:, 2, 512:576].rearrange("p (a s) -> p a s", a=4)
            if seg == 0:
                nat = scrf0[0:NS, 0:2048]
                for half in range(4):
                    r, hc = half // 2, half % 2
                    pr.dma("sp", nat, sconv_d[l, :, r, hc * 2048:(hc + 1) * 2048], writes=["scr0", "scr1", "scr2", "scr3"], dsem="smp")
                    tpt = tp[nxt("tp", 2)]
                    for qq in range(16):
                        pr.op("pe", lambda e, tpt=tpt, qq=qq: e.transpose(tpt[:, qq * NS:(qq + 1) * NS], nat[:, qq * 128:(qq + 1) * 128], ident[0:NS, 0:NS]),
                              reads=["scr0", "scr1", "scr2", "scr3", "ident"], writes=[tpt.name], inc=(qq == 15))
                    pr.op("dve", lambda e, tpt=tpt, r=r, hc=hc: e.tensor_copy(stc[:, hc * 16:(hc + 1) * 16, r, :],
                                                                              tpt[:, 0:16 * NS].rearrange("p (q s) -> p q s", q=16)),
                          reads=[tpt.name], writes=["smp0", "smp1"])
                pr.dma("sp", ncs_o[l, :, 0, :], sconv_d[l, :, 1, :], dsem="ycopy")
            for q in range(QI):
                smt = sm[nxt("sm", 2)]
                banks = {}
                outs = {}
                for pi, name in enumerate(["c", "v", "b", "z"]):
                    col = ({"b": 0, "c": 1, "v": 2, "z": 3}[name]) * DI + q * 128
                    bank = nxt("mm", 4)
                    outs[name] = mm_chunk(w_in[:, col:col + 128], lambda k, c0, n: hT_t[:, k, c0:c0 + n], ["hT"], seg, bank, smt, pi * NS)
                (oc, rc, _, _), (ov, rv, _, _) = outs["c"][0], outs["v"][0]
                (ob, rb, _, _), (oz, rz, _, _) = outs["b"][0], outs["z"][0]
                pr.op("act", lambda e, q=q: e.copy(ue[:, 0:2], ucar[:, l, q, :]), reads=["ucar"], writes=["scr0"])
                pr.op("act", lambda e, oc=oc: e.copy(ctmp, oc), reads=[rc], writes=["scr1"])
                pr.op("dve", lambda e, ov=ov: e.tensor_tensor(ue[:, 2:NT + 2], ctmp, ov, ALU.mult), reads=["scr1", rv], writes=["scr0"])
                pr.op("act", lambda e, q=q: e.copy(ucar[:, l, q, :], ue[:, NT:NT + 2]), reads=["scr0"], writes=["ucar"])
                pr.op("dve", lambda e, q=q: e.tensor_scalar(cv, ue[:, 0:NT], convw[:, l, 0, q:q + 1], None, ALU.mult), reads=["scr0", "convw"], writes=["scr2"])
                pr.op("dve", lambda e, q=q: e.scalar_tensor_tensor(cv, ue[:, 1:NT + 1], convw[:, l, 1, q:q + 1], cv, ALU.mult, ALU.add),
                      reads=["scr0", "scr2", "convw"], writes=["scr2"])
                pr.op("dve", lambda e, q=q: e.scalar_tensor_tensor(cv, ue[:, 2:NT + 2], convw[:, l, 2, q:q + 1], cv, ALU.mult, ALU.add),
                      reads=["scr0", "scr2", "convw"], writes=["scr2"])
                pr.op("act", lambda e, oz=oz: e.activation(out=sz, in_=oz, func=AF.Silu), reads=[rz], writes=["scr3"])
                pr.op("dve", lambda e, ob=ob: e.tensor_tensor(sz, sz, ob, ALU.mult), reads=["scr3", rb], writes=["scr3"])
                pr.op("dve", lambda e, q=q: e.tensor_tensor(A_t[:, q, 0:NT], sz, cv, ALU.mult), reads=["scr3", "scr2"], writes=["A"])
                if seg == 0:
                    sres = [smt.name] * 4
                    pr.op("act", lambda e, smt=smt: e.copy(stmp[:, 0, :], smt[:, 0:NS]), reads=[sres[0]], writes=["smp3"])
                    pr.op("dve", lambda e, smt=smt, q=q: e.tensor_tensor(us[:, q, :], stmp[:, 0, :], smt[:, NS:2 * NS], ALU.mult),
                          reads=["smp3", sres[1]], writes=["smp2"])
                    pr.op("dve", lambda e, q=q: e.tensor_scalar(stmp[:, 1, :], stc[:, q, 0, :], convw[:, l, 0, q:q + 1], None, ALU.mult),
                          reads=["smp0", "smp1", "convw"], writes=["smp3"])
                    pr.op("dve", lambda e, q=q: e.scalar_tensor_tensor(stmp[:, 1, :], stc[:, q, 1, :], convw[:, l, 1, q:q + 1], stmp[:, 1, :], ALU.mult, ALU.add),
                          reads=["smp0", "smp1", "smp3", "convw"], writes=["smp3"])
                    pr.op("dve", lambda e, q=q: e.scalar_tensor_tensor(stmp[:, 1, :], us[:, q, :], convw[:, l, 2, q:q + 1], stmp[:, 1, :], ALU.mult, ALU.add),
                          reads=["smp2", "smp3", "convw"], writes=["smp3"])
                    pr.op("act", lambda e, smt=smt: e.activation(out=stmp[:, 2, :], in_=smt[:, 3 * NS:4 * NS], func=AF.Silu), reads=[sres[3]], writes=["smp3"])
                    pr.op("dve", lambda e, smt=smt: e.tensor_tensor(stmp[:, 2, :], stmp[:, 2, :], smt[:, 2 * NS:3 * NS], ALU.mult),
                          reads=["smp3", sres[2]], writes=["smp3"])
                    pr.op("dve", lambda e, q=q: e.tensor_tensor(A_t[:, q, NT:NX], stmp[:, 2, :], stmp[:, 1, :], ALU.mult), reads=["smp3"], writes=["A"])
            if seg == 0:
                stage = smp[0:NS, 0, :]
                for part in range(8):
                    ti = nxt("tp", 2)
                    tpt = tp[ti]
                    for qq in range(4):
                        q = part * 4 + qq
                        pr.op("pe", lambda e, tpt=tpt, qq=qq, q=q: e.transpose(tpt[0:NS, qq * 128:(qq + 1) * 128], us[:, q, :], ident[:]),
                              reads=["smp2", "ident"], writes=["tp%d" % ti], inc=(qq == 3))
                    pr.op("act", lambda e, tpt=tpt: e.copy(stage[:, 0:512], tpt[0:NS, 0:512]), reads=["tp%d" % ti], writes=["smp0"])
                    pr.dma("sp", ncs_o[l, :, 1, part * 512:(part + 1) * 512], stage[:, 0:512], reads=["smp0"], dsem="ysmp")
            if dump_x and seg == 0:
                pr.dma("sp", dbgA_o[l], A_t[:, :, NT:NX], reads=["A"], dsem="dbg")
                pr.dma("sp", dbgS_o[l], stmp, reads=["smp3"], dsem="dbg")
            out_proj(conv_w_out_d[l], A_t, "A", seg, jmajor=False)
            if last_seg:
                stage2 = smp[0:2, 0, :]
                for part in range(8):
                    tpt = tp[nxt("tp", 2)]
                    for qq in range(4):
                        q = part * 4 + qq
                        pr.op("pe", lambda e, tpt=tpt, qq=qq, q=q: e.transpose(tpt[0:2, qq * 128:(qq + 1) * 128], ucar[:, l, q, :], ident[:]),
                              reads=["ucar", "ident"], writes=[tpt.name], inc=(qq == 3))
                    pr.op("act", lambda e, tpt=tpt: e.copy(stage2[:, 0:512], tpt[0:2, 0:512]), reads=[tpt.name], writes=["smp0"])
                    pr.dma("sp", ncp_o[l, :, part * 512:(part + 1) * 512], stage2[:, 0:512], reads=["smp0"], dsem="ysmp")

        from types import SimpleNamespace
        ctx = SimpleNamespace(**{k: v for k, v in locals().items() if not k.startswith("__")})
        ssm_layer = make_ssm_layer(ctx)
        for seg in range(nseg):
            load_x(seg)
            li = 0
            for layer in range(nlayers):
                if layer % 2 == 0:
                    conv_layer(layer // 2, seg, seg == nseg - 1)
                elif not skip_ssm:
                    ssm_layer(layer // 2, seg, seg == nseg - 1)
                if dump_x and seg == 0:
                    pr.dma("sp", dbg_o[layer], x_t[:], reads=["x"], dsem="dbg")
            final_out(seg)
        pr.final_wait("sp")
        pr.emit()
    return nc


def make_ssm_layer(c):
    pr, nc, sb, nxt = c.pr, c.nc, c.sb, c.nxt
    tp, sm, mm = c.tp, c.sm, c.mm
    ident, identb, maskT = c.ident, c.identb, c.maskT
    x_t, hT_t, A_t, B_t, uh_v, uhs, y3_v = c.x_t, c.hT_t, c.A_t, c.B_t, c.uh_v, c.uhs, c.y3_v
    scr, smp, Xfin, dgm, bglu = c.scr, c.smp, c.Xfin, c.dgm, c.bglu
    scrf = scr[:].rearrange("p a b -> p (a b)")
    SCR = ["scr0", "scr1", "scr2", "scr3"]

    aT = sb("aT", [128, 2, 128])
    NTAB = 26
    tabs = sb("tabs", [128, NTAB, 128])
    WV = sb("WV", [128, 4, 128, 8], BF16)
    BC = sb("BC", [128, 4, 32, 16], BF16)
    PQ = sb("PQ", [128, 2, 4, 8, 128], BF16)
    TP = sb("TP", [128, 2, 2, 128], BF16)
    R_all = A_t[:].rearrange("p a b -> p (a b)")[:, 0:2 * 128 * NCH].rearrange("p (r g c) -> p r g c", r=2, g=128)
    hTf = hT_t[:].rearrange("p a b -> p (a b)")
    Rs_all = hTf[:, 0:4096].rearrange("p (r g s) -> p r g s", r=2, g=128)
    xsp_all = hTf[:, 4096:8192].rearrange("p (r g s) -> p r g s", r=2, g=128)

    T_ = lambda i: tabs[:, i, :]
    U_ = T_
    (I_DT, I_PHI, I_RA, I_RHO, I_RHOI, I_S, I_C, I_T1, I_T2, I_LR, I_LI, I_NR, I_NI, I_KR,
     J_KI, J_M1R, J_M1I, J_L7R, J_L7I, J_X1, J_X2, J_X3, J_X4, J_X5, I_A8R, I_A8I) = range(26)

    def dve(fn, reads, writes):
        pr.op("dve", fn, reads=reads, writes=writes)

    def act(fn, reads, writes):
        pr.op("act", fn, reads=reads, writes=writes)

    def cmul(o_r, o_i, a_r, a_i, b_r, b_i, t1, t2, res_r, res_w, tres):
        dve(lambda e: e.tensor_tensor(t1, a_r, b_r, ALU.mult), res_r, tres)
        dve(lambda e: e.tensor_tensor(t2, a_i, b_i, ALU.mult), res_r, tres)
        dve(lambda e: e.tensor_tensor(o_r, t1, t2, ALU.subtract), tres, res_w)
        dve(lambda e: e.tensor_tensor(t1, a_r, b_i, ALU.mult), res_r, tres)
        dve(lambda e: e.tensor_tensor(t2, a_i, b_r, ALU.mult), res_r, tres)
        dve(lambda e: e.tensor_tensor(o_i, t1, t2, ALU.add), tres, res_w)

    def load_params(l, seg):
        nat = scrf[:, 0:256].rearrange("p (t x) -> p t x", t=2)
        for t, src in enumerate([c.a_re_d, c.a_im_d]):
            pr.dma("sp", nat[:, t, :].rearrange("p (gh q) -> p gh q", gh=2), src[l].rearrange("(gh gp) p -> gp gh p", gh=2),
                   writes=SCR, dsem="spar")
        pr.seal("spar", SCR)
        tpt = tp[nxt("tp", 2)]
        for t in range(2):
            pr.op("pe", lambda e, t=t, tpt=tpt: e.transpose(tpt[:, t * 128:(t + 1) * 128], nat[:, t, :], ident[:]),
                  reads=SCR + ["ident"], writes=[tpt.name], inc=(t == 1))
        dve(lambda e, tpt=tpt: e.tensor_copy(aT[:].rearrange("p a b -> p (a b)"), tpt[:, 0:256]), [tpt.name], ["aT"])
        if SSM_STOP < 1.1:
            return
        ldrow = scrf[0:1, 512:768]
        pr.dma("sp", ldrow, c.ldt_d[l:l + 1, :], writes=["scr1"], dsem="spar")
        pr.seal("spar", ["scr1"])
        pst = sm[nxt("sm", 2)]
        pr.op("pe", lambda e, pst=pst: e.matmul(pst[:, 0:256], c.ones_f[0:1, :], ldrow, start=True, stop=True),
              reads=["scr1", "ones_f"], writes=[pst.name])
        for gh in range(2):
            dve(lambda e, pst=pst, gh=gh: e.tensor_copy(tabs[64 * gh:64 * gh + 64, I_DT, :], pst[64 * gh:64 * gh + 64, gh * 128:(gh + 1) * 128]),
                [pst.name], ["tabs"])
        TB, TB2 = ["tabs"], ["tabs"]
        if SSM_STOP < 1.2:
            return
        aTr, aTi = aT[:, 0, :], aT[:, 1, :]
        act(lambda e: e.activation(out=T_(I_DT), in_=T_(I_DT), func=AF.Exp), TB, TB)
        dve(lambda e: e.tensor_tensor(T_(I_PHI), T_(I_DT), aTi, ALU.mult), TB + ["aT"], TB)
        dve(lambda e: e.tensor_tensor(T_(I_RA), T_(I_DT), aTr, ALU.mult), TB + ["aT"], TB)
        act(lambda e: e.activation(out=T_(I_RHO), in_=T_(I_RA), func=AF.Exp), TB, TB)
        act(lambda e: e.activation(out=T_(I_RHOI), in_=T_(I_RA), func=AF.Exp, scale=-1.0), TB, TB)
        act(lambda e: e.activation(out=T_(I_T1), in_=T_(I_PHI), func=AF.Sin, scale=1.0 / 16), TB, TB)
        act(lambda e: e.activation(out=T_(I_S), in_=T_(I_PHI), func=AF.Sin, scale=1.0 / 8), TB, TB)
        dve(lambda e: e.tensor_tensor(T_(I_T2), T_(I_T1), T_(I_T1), ALU.mult), TB, TB)
        dve(lambda e: e.tensor_scalar(T_(I_C), T_(I_T2), -2.0, 1.0, ALU.mult, ALU.add), TB, TB)
        for _ in range(3):
            dve(lambda e: e.tensor_tensor(T_(I_T1), T_(I_S), T_(I_C), ALU.mult), TB, TB)
            dve(lambda e: e.tensor_tensor(T_(I_T2), T_(I_S), T_(I_S), ALU.mult), TB, TB)
            dve(lambda e: e.tensor_scalar(T_(I_S), T_(I_T1), 2.0, None, ALU.mult), TB, TB)
            dve(lambda e: e.tensor_scalar(T_(I_C), T_(I_T2), -2.0, 1.0, ALU.mult, ALU.add), TB, TB)
        dve(lambda e: e.tensor_tensor(T_(I_LR), T_(I_RHO), T_(I_C), ALU.mult), TB, TB)
        dve(lambda e: e.tensor_tensor(T_(I_LI), T_(I_RHO), T_(I_S), ALU.mult), TB, TB)
        dve(lambda e: e.tensor_tensor(U_(J_M1R), T_(I_RHOI), T_(I_C), ALU.mult), TB, TB2)
        dve(lambda e: e.scalar_tensor_tensor(U_(J_M1I), T_(I_RHOI), -1.0, T_(I_S), ALU.mult, ALU.mult), TB, TB2)
        dve(lambda e: e.tensor_scalar(U_(J_X1), T_(I_LR), -1.0, None, ALU.add), TB, TB2)
        dve(lambda e: e.tensor_tensor(U_(J_X2), U_(J_X1), aTr, ALU.mult), TB2 + ["aT"], TB2)
        dve(lambda e: e.tensor_tensor(U_(J_X3), T_(I_LI), aTi, ALU.mult), TB + ["aT"], TB2)
        dve(lambda e: e.tensor_tensor(T_(I_NR), U_(J_X2), U_(J_X3), ALU.add), TB2, TB)
        dve(lambda e: e.tensor_tensor(U_(J_X2), T_(I_LI), aTr, ALU.mult), TB + ["aT"], TB2)
        dve(lambda e: e.tensor_tensor(U_(J_X3), U_(J_X1), aTi, ALU.mult), TB2 + ["aT"], TB2)
        dve(lambda e: e.tensor_tensor(T_(I_NI), U_(J_X2), U_(J_X3), ALU.subtract), TB2, TB)
        dve(lambda e: e.tensor_tensor(U_(J_X2), aTr, aTr, ALU.mult), ["aT"], TB2)
        dve(lambda e: e.tensor_tensor(U_(J_X3), aTi, aTi, ALU.mult), ["aT"], TB2)
        dve(lambda e: e.tensor_tensor(U_(J_X2), U_(J_X2), U_(J_X3), ALU.add), TB2, TB2)
        dve(lambda e: e.reciprocal(U_(J_X3), U_(J_X2)), TB2, TB2)
        dve(lambda e: e.tensor_tensor(T_(I_KR), T_(I_NR), U_(J_X3), ALU.mult), TB + TB2, TB)
        dve(lambda e: e.tensor_tensor(U_(J_KI), T_(I_NI), U_(J_X3), ALU.mult), TB + TB2, TB2)
        if SSM_STOP < 1.3:
            return
        t1, t2 = U_(J_X4), U_(J_X5)
        pp = [(T_(I_PHI), T_(I_RA)), (T_(I_RHO), T_(I_RHOI))]
        cr, ci = T_(I_LR), T_(I_LI)
        for j in range(8):
            if j > 0:
                nr, ni = pp[j % 2] if j < 7 else (T_(I_A8R), T_(I_A8I))
                cmul(nr, ni, cr, ci, T_(I_LR), T_(I_LI), t1, t2, TB, TB, TB)
                cr, ci = nr, ni
            dve(lambda e, j=j, cr=cr: e.tensor_copy(WV[:, 2, :, j], cr), TB, ["WV"])
            dve(lambda e, j=j, ci=ci: e.tensor_copy(WV[:, 3, :, j], ci), TB, ["WV"])
        if SSM_STOP < 1.4:
            return
        cr, ci = None, None
        for j in range(8):
            nr, ni = pp[j % 2]
            if j == 0:
                cmul(nr, ni, U_(J_M1R), U_(J_M1I), T_(I_KR), U_(J_KI), t1, t2, TB, TB, TB)
            else:
                cmul(nr, ni, cr, ci, U_(J_M1R), U_(J_M1I), t1, t2, TB, TB, TB)
            cr, ci = nr, ni
            dve(lambda e, j=j, cr=cr: e.tensor_copy(WV[:, 0, :, j], cr), TB, ["WV"])
            dve(lambda e, j=j, ci=ci: e.tensor_copy(WV[:, 1, :, j], ci), TB, ["WV"])
        if SSM_STOP < 1.5:
            return
        if seg == 0:
            m1r, m1i = U_(J_M1R), U_(J_M1I)
            m2r, m2i, m3r, m3i, m4r, m4i = T_(I_NR), T_(I_NI), T_(I_T1), T_(I_T2), T_(I_S), T_(I_C)
            cmul(m2r, m2i, m1r, m1i, m1r, m1i, t1, t2, ["tabs"], ["tabs"], TB2)
            cmul(m4r, m4i, m2r, m2i, m2r, m2i, t1, t2, ["tabs"], ["tabs"], TB2)
            cmul(m3r, m3i, m2r, m2i, m1r, m1i, t1, t2, ["tabs", "tabs"], ["tabs"], TB2)
            cmul(U_(J_L7R), U_(J_L7I), m4r, m4i, m3r, m3i, t1, t2, ["tabs"], ["tabs"], TB2)

    def load_bc(l, qt):
        g0 = 32 * qt
        bst = scrf[:, 0:512].rearrange("p (g h) -> p g h", h=16)
        for t, src in enumerate([c.b_re_d, c.b_im_d]):
            for gh in range(2):
                pr.dma("sp", bst[64 * gh:64 * gh + 64, :, :], src[l, gh * 128 + g0:gh * 128 + g0 + 32].rearrange("g p h -> p g h"),
                       writes=["scr0"], dsem="spar")
            pr.seal("spar", ["scr0"])
            act(lambda e, t=t: e.copy(BC[:, t, :, :], bst), ["scr0"], ["BC"])
        cst = scrf[:, 1032:1032 + 512].rearrange("p (gb x) -> p gb x", gb=4)
        for t, src in enumerate([c.c_re_d, c.c_im_d]):
            sv = src[l].rearrange("(gh gq gb gl) h p -> gq (gl h) gb gh p", gh=2, gq=4, gb=4)
            for gh in range(2):
                pr.dma("sp", cst[:, :, gh * 64:(gh + 1) * 64], sv[qt][:, :, gh, :], writes=["scr2"], dsem="spar")
            pr.seal("spar", ["scr2"])
            tpt = tp[nxt("tp", 2)]
            for k in range(4):
                pr.op("pe", lambda e, tpt=tpt, k=k: e.transpose(tpt[:, k * 128:(k + 1) * 128], cst[:, k, :], ident[:]),
                      reads=["scr2", "ident"], writes=[tpt.name], inc=(k == 3))
            act(lambda e, tpt=tpt, t=t: e.copy(BC[:, 2 + t, :, :].rearrange("p g h -> p (g h)"), tpt[:, 0:512]), [tpt.name], ["BC"])

    def gen_tables(sub, buf, want_q):
        g0 = sub * 8
        t1 = scrf[:, 0:1024].rearrange("p (g j h) -> p g j h", g=8, j=8)
        t2 = scrf[:, 1024:2048].rearrange("p (g j h) -> p g j h", g=8, j=8)
        res_pq = ("PQ", buf)

        def tab(i):
            return WV[:, i, g0:g0 + 8, :].unsqueeze(3).broadcast_to([128, 8, 8, 16])

        gq0 = (sub % 4) * 8

        def bc(i):
            return BC[:, i, gq0:gq0 + 8, :].unsqueeze(2).broadcast_to([128, 8, 8, 16])

        def out(i):
            return PQ[:, buf, i, :, :].rearrange("p g (j h) -> p g j h", j=8)
        rd = ["WV", "BC"]
        dve(lambda e: e.tensor_tensor(t1, tab(0), bc(0), ALU.mult), rd, ["scr0", "scr1"])
        dve(lambda e: e.tensor_tensor(t2, tab(1), bc(1), ALU.mult), rd, ["scr2", "scr3"])
        dve(lambda e: e.tensor_tensor(out(0), t1, t2, ALU.subtract), SCR, [res_pq])
        dve(lambda e: e.tensor_tensor(t1, tab(0), bc(1), ALU.mult), rd, ["scr0", "scr1"])
        dve(lambda e: e.tensor_tensor(t2, tab(1), bc(0), ALU.mult), rd, ["scr2", "scr3"])
        dve(lambda e: e.tensor_tensor(out(1), t1, t2, ALU.add), SCR, [res_pq])
        if want_q:
            dve(lambda e: e.tensor_tensor(t1, tab(2), bc(2), ALU.mult), rd, ["scr0", "scr1"])
            dve(lambda e: e.tensor_tensor(t2, tab(3), bc(3), ALU.mult), rd, ["scr2", "scr3"])
            dve(lambda e: e.tensor_tensor(out(2), t1, t2, ALU.subtract), SCR, [res_pq])
            dve(lambda e: e.tensor_tensor(t1, tab(2), bc(3), ALU.mult), rd, ["scr0", "scr1"])
            dve(lambda e: e.tensor_tensor(t2, tab(3), bc(2), ALU.mult), rd, ["scr2", "scr3"])
            dve(lambda e: e.tensor_tensor(t1, t1, t2, ALU.add), SCR, ["scr0", "scr1"])
            dve(lambda e: e.tensor_scalar(PQ[:, buf, 3, :, :].rearrange("p g x -> p (g x)"), scrf[:, 0:1024], -1.0, None, ALU.mult), ["scr0", "scr1"], [res_pq])

    def ssm_core(l, seg, last_seg):
        ncol_s = NS if seg == 0 else 0
        if seg == 0:
            pr.op("pool", lambda e: e.memset(uhs[0:112, :, :], 0.0), writes=["uhs"])
        Av = A_t[:, :, 0:NT].rearrange("p q (j c) -> p q j c", j=8)
        uhq = uh_v.rearrange("p (q gl) c -> p q gl c", gl=8)
        uhsq = uhs[:].rearrange("p (q gl) s -> p q gl s", gl=8)
        for gl in range(8):
            for j in range(8):
                pr.dma("sp", uhq[16 * j:16 * j + 16, :, gl, :], Av[16 * gl:16 * gl + 16, :, j, :], reads=["A"], writes=["B"], dsem="shuf")
            if seg == 0:
                pr.dma("sp", uhsq[112:128, :, gl, :], A_t[16 * gl:16 * gl + 16, :, NT:NX], reads=["A"], writes=["uhs"], dsem="shuf")
        pr.seal("shuf", ["B", "uhs"] if seg == 0 else ["B"])
        if SSM_STOP < 1:
            return
        load_params(l, seg)
        if SSM_STOP < 2:
            return

        for sub in range(16):
            buf = sub % 2
            if sub % 4 == 0:
                load_bc(l, sub // 4)
            if SSM_STOP < 2.1:
                return
            gen_tables(sub, buf, want_q=False)
            if SSM_STOP < 2.2:
                return
            for gs in range(8):
                gp = sub * 8 + gs
                rt = sm[nxt("sm", 2)]
                rtv = rt[:, 0:2 * NCH].rearrange("p (r c) -> p r c", r=2)
                rts = rt[:, 256:256 + 2 * NS].rearrange("p (r s) -> p r s", r=2)
                for gh in range(2):
                    g = gh * 128 + gp
                    lo, hi = 64 * gh, 64 * gh + 64
                    tb = nxt("TP", 2)
                    pt = tp[nxt("tp", 2)]
                    for r in range(2):
                        pr.op("pe", lambda e, pt=pt, r=r, lo=lo, hi=hi, buf=buf, gs=gs: e.matmul(pt[:, r * 64:(r + 1) * 64], PQ[lo:hi, buf, r, gs, :], identb[lo:hi, lo:hi], start=True, stop=True),
                              reads=[("PQ", buf), "identb"], writes=[pt.name], inc=(r == 1))
                    act(lambda e, pt=pt, tb=tb: e.copy(TP[:, tb, 1, :], pt[:, 0:128]), [pt.name], [("TP", tb)])
                    for r in range(2):
                        pr.op("pe", lambda e, rtv=rtv, r=r, lo=lo, hi=hi, tb=tb, g=g: e.matmul(rtv[lo:hi, r, :], TP[:, tb, 1, r * 64:(r + 1) * 64], uh_v[:, g, :], start=True, stop=True),
                              reads=[("TP", tb), "B"], writes=[rt.name], inc=(r == 1 and seg != 0))
                        if seg == 0:
                            pr.op("pe", lambda e, rts=rts, r=r, lo=lo, hi=hi, tb=tb, g=g: e.matmul(rts[lo:hi, r, :], TP[:, tb, 1, r * 64:(r + 1) * 64], uhs[:, g, :], start=True, stop=True),
                                  reads=[("TP", tb), "uhs"], writes=[rt.name], inc=(r == 1))
                dve(lambda e, rtv=rtv, gp=gp: e.tensor_copy(R_all[:, :, gp, :], rtv), [rt.name], ["A"])
                if seg == 0:
                    dve(lambda e, rts=rts, gp=gp: e.tensor_copy(Rs_all[:, :, gp, :], rts), [rt.name], ["hT"])
                if SSM_STOP < 2.3:
                    return

        if SSM_STOP < 3:
            return
        a8r, a8i = T_(I_A8R), T_(I_A8I)
        cur = Xfin[:, l, :, :]
        st_t = tabs[:, J_X1:J_X1 + 2, :]
        st_m = tabs[:, J_X3:J_X3 + 2, :]
        for cc in range(NCH):
            dve(lambda e, cc=cc: e.tensor_tensor(st_t, cur, R_all[:, :, :, cc], ALU.add), ["Xfin", "A"], ["tabs"])
            act(lambda e, cc=cc: e.copy(R_all[:, :, :, cc], cur), ["Xfin"], ["A"])
            dve(lambda e: e.tensor_tensor(st_m, st_t, a8r.unsqueeze(1).broadcast_to([128, 2, 128]), ALU.mult), ["tabs"], ["tabs"])
            dve(lambda e: e.tensor_tensor(st_t[:, 0, :], st_t[:, 0, :], a8i, ALU.mult), ["tabs"], ["tabs"])
            dve(lambda e: e.tensor_tensor(st_t[:, 1, :], st_t[:, 1, :], a8i, ALU.mult), ["tabs"], ["tabs"])
            dve(lambda e: e.tensor_tensor(cur[:, 0, :], st_m[:, 0, :], st_t[:, 1, :], ALU.subtract), ["tabs"], ["Xfin"])
            dve(lambda e: e.tensor_tensor(cur[:, 1, :], st_m[:, 1, :], st_t[:, 0, :], ALU.add), ["tabs"], ["Xfin"])

        if SSM_STOP < 4:
            return
        if seg == 0:
            sample_states(l)

        if SSM_STOP < 5:
            return
        for sub in range(16):
            buf = sub % 2
            if sub % 4 == 0:
                load_bc(l, sub // 4)
            gen_tables(sub, buf, want_q=True)
            for gs in range(8):
                gp = sub * 8 + gs
                for gh in range(2):
                    g = gh * 128 + gp
                    lo, hi = 64 * gh, 64 * gh + 64
                    tb = nxt("TP", 2)
                    pt = tp[nxt("tp", 2)]
                    for r in range(2):
                        pr.op("pe", lambda e, pt=pt, r=r, lo=lo, hi=hi, buf=buf, gs=gs: e.matmul(pt[:, 0:128], PQ[lo:hi, buf, r, gs, :], PQ[lo:hi, buf, 2 + r, gs, :], start=(r == 0), stop=(r == 1)),
                              reads=[("PQ", buf)], writes=[pt.name], inc=(r == 1))
                    dve(lambda e, pt=pt, tb=tb: e.tensor_tensor(TP[:, tb, 0, :], pt[:, 0:128], maskT[:].rearrange("p a b -> p (a b)"), ALU.mult),
                        [pt.name, "maskT"], [("TP", tb)])
                    yt = sm[nxt("sm", 2)]
                    pr.op("pe", lambda e, yt=yt, tb=tb, g=g: e.matmul(yt[:, 0:NCH], TP[:, tb, 0, :], uh_v[:, g, :], start=True, stop=False),
                          reads=[("TP", tb), "B"], writes=[yt.name], inc=False)
                    for r in range(2):
                        pr.op("pe", lambda e, yt=yt, r=r, lo=lo, hi=hi, buf=buf, gs=gs, gp=gp: e.matmul(yt[:, 0:NCH], PQ[lo:hi, buf, 2 + r, gs, :], R_all[lo:hi, r, gp, :], start=False, stop=(r == 1)),
                              reads=[("PQ", buf), "A"], writes=[yt.name], inc=(r == 1))
                    dve(lambda e, yt=yt, g=g: e.scalar_tensor_tensor(uh_v[:, g, :], uh_v[:, g, :], dgm[:, l, g:g + 1], yt[:, 0:NCH], ALU.mult, ALU.add),
                        [yt.name, "B", "dgm"], ["B"])
                    if seg == 0:
                        pr.op("pe", lambda e, yt=yt, tb=tb, g=g: e.matmul(yt[:, 128:128 + NS], TP[:, tb, 0, :], uhs[:, g, :], start=True, stop=False),
                              reads=[("TP", tb), "uhs"], writes=[yt.name], inc=False)
                        for r in range(2):
                            pr.op("pe", lambda e, yt=yt, r=r, lo=lo, hi=hi, buf=buf, gs=gs, gp=gp: e.matmul(yt[:, 128:128 + NS], PQ[lo:hi, buf, 2 + r, gs, :], xsp_all[lo:hi, r, gp, :], start=False, stop=(r == 1)),
                                  reads=[("PQ", buf), "hT"], writes=[yt.name], inc=(r == 1))
                        dve(lambda e, yt=yt, g=g: e.scalar_tensor_tensor(uhs[:, g, :], uhs[:, g, :], dgm[:, l, g:g + 1], yt[:, 128:128 + NS], ALU.mult, ALU.add),
                            [yt.name, "uhs", "dgm"], ["uhs"])

        if SSM_STOP < 6:
            return
        for gl in range(8):
            for j in range(8):
                pr.dma("sp", Av[16 * gl:16 * gl + 16, :, j, :], uhq[16 * j:16 * j + 16, :, gl, :], reads=["B"], writes=["A"], dsem="shuf")
            if seg == 0:
                pr.dma("sp", A_t[16 * gl:16 * gl + 16, :, NT:NX], uhsq[112:128, :, gl, :], reads=["uhs"], writes=["A"], dsem="shuf")
        pr.seal("shuf", ["A"])

        if last_seg:
            stg = smp[:, 0, 0:256].rearrange("p (r x) -> p r x", r=2)
            tpt = tp[nxt("tp", 2)]
            for r in range(2):
                pr.op("pe", lambda e, tpt=tpt, r=r: e.transpose(tpt[:, r * 128:(r + 1) * 128], Xfin[:, l, r, :], ident[:]),
                      reads=["Xfin", "ident"], writes=[tpt.name], inc=(r == 1))
            dve(lambda e, tpt=tpt: e.tensor_copy(stg, tpt[:, 0:256].rearrange("p (r x) -> p r x", r=2)), [tpt.name], ["smp0"])
            for r, dst in enumerate([c.nrp_o, c.nip_o]):
                pr.dma("sp", dst[l].rearrange("(gh gp) p -> gp gh p", gh=2), stg[:, r, :].rearrange("p (gh q) -> p gh q", gh=2), reads=["smp0"], dsem="ysmp")

    def sample_states(l):
        lam_r, lam_i = T_(I_LR), T_(I_LI)
        a8r, a8i = T_(I_A8R), T_(I_A8I)
        l7r, l7i = U_(J_L7R), U_(J_L7I)
        nat = smp[:, 0, 0:512].rearrange("p (sh r x) -> p sh r x", sh=2, r=2)
        x0q = smp[:, 1, 0:512].rearrange("p (r sh x) -> p r sh x", r=2, sh=2)
        xnq = smp[:, 2, 0:512].rearrange("p (r sh x) -> p r sh x", r=2, sh=2)
        t1 = scrf[:, 0:256].rearrange("p (sh sl gl) -> p sh sl gl", sh=2, sl=8)
        t2 = scrf[:, 256:512].rearrange("p (sh sl gl) -> p sh sl gl", sh=2, sl=8)
        for et in range(8):
            g0 = 16 * et
            for r, src in enumerate([c.sre_d, c.sim_d]):
                sv = src[l].rearrange("(sh sl) (gh ge gl) p -> sl ge gl sh gh p", sl=8, gh=2, ge=8)
                for sl in range(8):
                    for sh in range(2):
                        pr.dma("sp", nat[16 * sl:16 * sl + 16, sh, r, :].rearrange("p (gh q) -> p gh q", gh=2), sv[sl, et][:, sh],
                               writes=["smp0"], dsem="sst")
            pr.seal("sst", ["smp0"])
            for r in range(2):
                tpt = tp[nxt("tp", 2)]
                for sh in range(2):
                    pr.op("pe", lambda e, tpt=tpt, sh=sh, r=r: e.transpose(tpt[:, sh * 128:(sh + 1) * 128], nat[:, sh, r, :], ident[:]),
                          reads=["smp0", "ident"], writes=[tpt.name], inc=(sh == 1))
                dve(lambda e, tpt=tpt, r=r: e.tensor_copy(x0q[:, r, :, :], tpt[:, 0:256].rearrange("p (sh x) -> p sh x", sh=2)), [tpt.name], ["smp1"])

            def xv(r):
                return x0q[:, r, :, :].rearrange("p sh (sl gl) -> p sh sl gl", sl=8)

            def cf(tbl):
                return tbl[:, g0:g0 + 16].unsqueeze(1).unsqueeze(1).broadcast_to([128, 2, 8, 16])

            def spo(r):
                return xsp_all[:, r, g0:g0 + 16, :].rearrange("p gl (sh sl) -> p sh sl gl", sh=2)

            def rsv(r):
                return Rs_all[:, r, g0:g0 + 16, :].rearrange("p gl (sh sl) -> p sh sl gl", sh=2)

            def xno(r):
                return xnq[:, r, :, :].rearrange("p sh (sl gl) -> p sh sl gl", sl=8)
            RD = ["smp1", "tabs", "hT"]
            S01 = ["scr0"]

            def TT(o, a, b, op, rd, wr):
                dve(lambda e, o=o, a=a, b=b, op=op: e.tensor_tensor(o, a, b, op), rd, wr)
            TT(t1, xv(0), cf(l7r), ALU.mult, RD, S01)
            TT(t2, xv(1), cf(l7i), ALU.mult, RD, S01)
            TT(spo(0), t1, t2, ALU.subtract, S01, ["hT"])
            TT(t1, xv(0), cf(l7i), ALU.mult, RD, S01)
            TT(t2, xv(1), cf(l7r), ALU.mult, RD, S01)
            TT(spo(1), t1, t2, ALU.add, S01, ["hT"])
            TT(t1, xv(0), cf(lam_r), ALU.mult, RD, S01)
            TT(t2, xv(1), cf(lam_i), ALU.mult, RD, S01)
            TT(t1, t1, t2, ALU.subtract, S01, S01)
            TT(t2, rsv(0), cf(a8r), ALU.mult, RD, S01)
            TT(t1, t1, t2, ALU.add, S01, S01)
            TT(t2, rsv(1), cf(a8i), ALU.mult, RD, S01)
            TT(xno(0), t1, t2, ALU.subtract, S01, ["smp2"])
            TT(t1, xv(0), cf(lam_i), ALU.mult, RD, S01)
            TT(t2, xv(1), cf(lam_r), ALU.mult, RD, S01)
            TT(t1, t1, t2, ALU.add, S01, S01)
            TT(t2, rsv(0), cf(a8i), ALU.mult, RD, S01)
            TT(t1, t1, t2, ALU.add, S01, S01)
            TT(t2, rsv(1), cf(a8r), ALU.mult, RD, S01)
            TT(xno(1), t1, t2, ALU.add, S01, ["smp2"])
            for r, dst in enumerate([c.nrs_o, c.nis_o]):
                tpt = tp[nxt("tp", 2)]
                for sh in range(2):
                    pr.op("pe", lambda e, tpt=tpt, sh=sh, r=r: e.transpose(tpt[:, sh * 128:(sh + 1) * 128], xnq[:, r, sh, :], ident[:]),
                          reads=["smp2", "ident"], writes=[tpt.name], inc=(sh == 1))
                dve(lambda e, tpt=tpt, r=r: e.tensor_copy(nat[:, :, r, :], tpt[:, 0:256].rearrange("p (sh x) -> p sh x", sh=2)), [tpt.name], ["smp0"])
                dv = dst[l].rearrange("(sh sl) (gh ge gl) p -> sl ge gl sh gh p", sl=8, gh=2, ge=8)
                for sl in range(8):
                    for sh in range(2):
                        pr.dma("sp", dv[sl, et][:, sh], nat[16 * sl:16 * sl + 16, sh, r, :].rearrange("p (gh q) -> p gh q", gh=2),
                               reads=["smp0"], dsem="sst")
            pr.seal("sst", [])

    def ssm_layer(l, seg, last_seg):
        gi = 2 * l + 1
        n = NX if seg == 0 else NT
        c.rmsnorm(gi, seg, jmajor=True)
        w_in = c.ssm_w_in_d[l]
        for q in range(QI):
            bank = nxt("mm", 4)
            smt = sm[nxt("sm", 2)]
            outs = c.mm_chunk(w_in[:, q * 128:(q + 1) * 128], lambda k, c0, nn: hT_t[:, k, c0:c0 + nn], ["hT"], seg, bank, smt, 0)
            for (o, res, c0, nn) in outs:
                act(lambda e, o=o, q=q, c0=c0, nn=nn: e.copy(A_t[:, q, c0:c0 + nn], o), [res], ["A"])
        ssm_core(l, seg, last_seg)
        for q in range(QI):
            yv = A_t[:, q, 0:n]
            g1 = scrf[:, 0:n]
            g2 = scrf[:, 1032:1032 + n]
            dve(lambda e, yv=yv, g1=g1: e.tensor_tensor(g1, yv, yv, ALU.mult), ["A"], ["scr0", "scr1"])
            dve(lambda e, g1=g1: e.tensor_scalar(g1, g1, 0.044715, 1.0, ALU.mult, ALU.add), ["scr0", "scr1"], ["scr0", "scr1"])
            dve(lambda e, yv=yv, g1=g1, g2=g2: e.tensor_tensor(g2, g1, yv, ALU.mult), ["scr0", "scr1", "A"], ["scr2", "scr3"])
            act(lambda e, g2=g2: e.activation(out=g2, in_=g2, func=AF.Sigmoid, scale=1.5957691216057308), ["scr2", "scr3"], ["scr2", "scr3"])
            dve(lambda e, yv=yv, g2=g2: e.tensor_tensor(yv, yv, g2, ALU.mult), ["scr2", "scr3", "A"], ["A"])
        if seg == 0:
            c.rmsnorm(gi, seg, jmajor=True)
        for m in range(QI):
            bank_g = nxt("mm", 4)
            smt = sm[nxt("sm", 2)]
            outs_g = c.mm_chunk(c.w_glu_d[l][:, m * 128:(m + 1) * 128], lambda k, c0, nn: A_t[:, k, c0:c0 + nn], ["A"], seg, bank_g, smt, 0)
            bank_z = nxt("mm", 4)
            outs_z = c.mm_chunk(w_in[:, DI + m * 128:DI + (m + 1) * 128], lambda k, c0, nn: hT_t[:, k, c0:c0 + nn], ["hT"], seg, bank_z, smt, 64)
            for (og, rg, c0, nn), (oz, rz, _, _) in zip(outs_g, outs_z):
                s1 = scrf[:, 0:nn]
                s2 = scrf[:, 1032:1032 + nn]
                act(lambda e, og=og, s1=s1, m=m: e.activation(out=s1, in_=og, func=AF.Sigmoid, bias=bglu[:, l, m:m + 1], scale=1.0), [rg, "bglu"], ["scr0", "scr1"])
                act(lambda e, oz=oz, s2=s2: e.activation(out=s2, in_=oz, func=AF.Silu), [rz], ["scr2", "scr3"])
                dve(lambda e, s1=s1, m=m, c0=c0, nn=nn: e.tensor_tensor(s1, s1, A_t[:, m, c0:c0 + nn], ALU.mult), ["scr0", "scr1", "A"], ["scr0", "scr1"])
                dve(lambda e, s1=s1, s2=s2, m=m, c0=c0, nn=nn: e.tensor_tensor(y3_v[:, m, c0:c0 + nn], s1, s2, ALU.mult), SCR, ["B"])
        c.out_proj(c.ssm_w_out_d[l], y3_v, "B", seg, jmajor=True)

    return ssm_layer


_CACHE = {}


def kernel(**inputs):
    inputs = {k: np.ascontiguousarray(np.asarray(v, dtype=np.float32)) for k, v in inputs.items()}
    if "nc" not in _CACHE:
        _CACHE["nc"] = build_program()
    nc = _CACHE["nc"]
    wnames = ["conv_norm", "conv_w_in", "conv_w", "conv_w_out", "ssm_norm", "ssm_w_in", "ssm_a_re", "ssm_a_im",
              "ssm_log_dt", "ssm_b_re", "ssm_b_im", "ssm_c_re", "ssm_c_im", "ssm_d", "ssm_w_glu", "ssm_b_glu",
              "ssm_w_out", "final_norm"]
    in_maps = []
    for i in range(NCORES):
        b = i % 4
        m = {w: inputs[w] for w in wnames}
        m["xp"] = inputs["x_prompt"][b]
        m["xs"] = np.ascontiguousarray(inputs["x_sample"][i * NS:(i + 1) * NS, 0, :])
        m["sconv"] = np.ascontiguousarray(inputs["state_conv"][:, i * NS:(i + 1) * NS])
        m["sre"] = np.ascontiguousarray(inputs["state_ssm_re"][:, i * NS:(i + 1) * NS])
        m["sim"] = np.ascontiguousarray(inputs["state_ssm_im"][:, i * NS:(i + 1) * NS])
        in_maps.append(m)
    res = run_bass_kernel_spmd(nc, in_maps, core_ids=list(range(NCORES)))
    R = res.results
    y_prompt = np.stack([R[b]["yp"] for b in range(4)], axis=0)
    y_sample = np.concatenate([R[i]["ys"] for i in range(NCORES)], axis=0)[:, None, :]
    ncp = np.stack([R[b]["ncp"] for b in range(4)], axis=1)
    ncs = np.concatenate([R[i]["ncs"] for i in range(NCORES)], axis=1)
    nrp = np.stack([R[b]["nrp"] for b in range(4)], axis=1)
    nip = np.stack([R[b]["nip"] for b in range(4)], axis=1)
    nrs = np.concatenate([R[i]["nrs"] for i in range(NCORES)], axis=1)
    nis = np.concatenate([R[i]["nis"] for i in range(NCORES)], axis=1)
    return (y_prompt.astype(np.float32), y_sample.astype(np.float32), ncp.astype(np.float32), ncs.astype(np.float32),
            nrp.astype(np.float32), nip.astype(np.float32), nrs.astype(np.float32), nis.astype(np.float32))
```

```python
import contextlib
import numpy as np
import concourse.bass as bass
import concourse.mybir as mybir
from concourse.bass_utils import run_bass_kernel_spmd

F32 = mybir.dt.float32
BF16 = mybir.dt.bfloat16
AF = mybir.ActivationFunctionType
ALU = mybir.AluOpType

D = 2048
DI = 4096
KD = 16
QI = 32
NT = 512
NS = 16
NX = NT + NS
NCH = 64
G = 256
P = 64
H = 16
SEQ = 2048
NSEG = 4
EPS = 1e-6
NCORES = 8
GB = 16
NBATCH = 128 // GB
SAME_ENGINE_SYNC = True
SSM_STOP = 99


class PT:
    def __init__(self, t, name):
        self.t = t
        self.name = name

    def __getitem__(self, k):
        return self.t[k]


class Prog:
    ENG = ("pe", "act", "dve", "pool", "sp")

    def __init__(self, nc, es):
        self.nc = nc
        self.es = es
        self.streams = {e: [] for e in self.ENG}
        self.esem = {}
        self.ecount = {e: 0 for e in self.ENG}
        self.dsems = {}
        self.waited = {e: {} for e in self.ENG}
        self.lastw = {}
        self.readers = {}
        self.sems = {}
        self.n_ops = 0
        self.sym = {e: [] for e in self.ENG}

    def _sem(self, key):
        if key not in self.sems:
            self.sems[key] = self.es.enter_context(self.nc.semaphore("s_%s_%s" % key))
        return self.sems[key]

    def _wait(self, eng, tok):
        key, val = tok
        if key == ("e", eng) and not (SAME_ENGINE_SYNC and eng != "pe"):
            return
        if self.waited[eng].get(key, 0) >= val:
            return
        self.waited[eng][key] = val
        sem = self._sem(key)
        self.streams[eng].append(lambda e, sem=sem, val=val: e.wait_ge(sem, val))
        self.sym[eng].append(("wait", key, val))

    def _deps(self, eng, reads, writes):
        for r in reads:
            t = self.lastw.get(r)
            if t is not None:
                self._wait(eng, t)
        for w in writes:
            t = self.lastw.get(w)
            if t is not None:
                self._wait(eng, t)
            for t in list(self.readers.get(w, {}).items()):
                self._wait(eng, t)

    def _commit(self, tok, reads, writes):
        for w in writes:
            self.lastw[w] = tok
            self.readers[w] = {}
        for r in reads:
            if r in writes:
                continue
            d = self.readers.setdefault(r, {})
            if d.get(tok[0], 0) < tok[1]:
                d[tok[0]] = tok[1]

    PSUM_RES = frozenset(["mm0", "mm1", "mm2", "mm3", "sm0", "sm1", "tp0", "tp1"])

    def op(self, eng, fn, reads=(), writes=(), inc=True):
        self.n_ops += 1
        xr = [r for r in reads if r in self.PSUM_RES and r not in writes]
        if xr:
            writes = list(writes) + xr
        self._deps(eng, reads, writes)
        key = ("e", eng)
        if inc:
            self.ecount[eng] += 1
            val = self.ecount[eng]
            sem = self._sem(key)
            self.streams[eng].append(lambda e, fn=fn, sem=sem: fn(e).then_inc(sem, 1))
            self.sym[eng].append(("inc", key, 1))
        else:
            val = self.ecount[eng] + 1
            self.streams[eng].append(lambda e, fn=fn: fn(e))
        self._commit((key, val), reads, writes)

    def dma(self, q, out, in_, reads=(), writes=(), dsem="d"):
        self.n_ops += 1
        self._deps(q, reads, writes)
        key = ("d", dsem)
        self.dsems[key] = self.dsems.get(key, 0) + 16
        val = self.dsems[key]
        sem = self._sem(key)
        self.streams[q].append(lambda e, out=out, in_=in_, sem=sem: e.dma_start(out=out, in_=in_).then_inc(sem, 16))
        self.sym[q].append(("inc", key, 16))
        self._commit((key, val), reads, writes)

    def seal(self, dsem, resources):
        key = ("d", dsem)
        for r in resources:
            self.lastw[r] = (key, self.dsems[key])

    def final_wait(self, eng):
        for key, val in self.dsems.items():
            self._wait(eng, (key, val))
        for e2 in self.ENG:
            if e2 != eng and self.ecount[e2] > 0:
                self._wait(eng, (("e", e2), self.ecount[e2]))

    def check_deadlock(self):
        val = {}
        pos = {e: 0 for e in self.ENG}
        progress = True
        while progress:
            progress = False
            for e in self.ENG:
                st = self.sym[e]
                while pos[e] < len(st):
                    kind, key, v = st[pos[e]]
                    if kind == "wait":
                        if val.get(key, 0) < v:
                            break
                    else:
                        val[key] = val.get(key, 0) + v
                    pos[e] += 1
                    progress = True
        stuck = {e: (pos[e], len(self.sym[e]), self.sym[e][pos[e]] if pos[e] < len(self.sym[e]) else None) for e in self.ENG}
        ok = all(pos[e] == len(self.sym[e]) for e in self.ENG)
        return ok, stuck, val

    def emit(self):
        nc = self.nc
        with nc.Block() as block:
            @block.tensor
            def _(e):
                for f in self.streams["pe"]:
                    f(e)

            @block.scalar
            def _(e):
                for f in self.streams["act"]:
                    f(e)

            @block.vector
            def _(e):
                for f in self.streams["dve"]:
                    f(e)

            @block.gpsimd
            def _(e):
                for f in self.streams["pool"]:
                    f(e)

            @block.sync
            def _(e):
                for f in self.streams["sp"]:
                    f(e)


def build_program(nseg=NSEG, nlayers=4, dump_x=False, skip_ssm=False):
    nc = bass.Bass("TRN2", target_bir_lowering=False)
    es = contextlib.ExitStack()

    def din(name, shape):
        return nc.dram_tensor(name, list(shape), F32, kind="ExternalInput").ap()

    def dout(name, shape):
        return nc.dram_tensor(name, list(shape), F32, kind="ExternalOutput").ap()

    xp_d = din("xp", [SEQ, D])
    xs_d = din("xs", [NS, D])
    sconv_d = din("sconv", [2, NS, 2, DI])
    sre_d = din("sre", [2, NS, G, P])
    sim_d = din("sim", [2, NS, G, P])
    conv_norm_d = din("conv_norm", [2, D])
    conv_w_in_d = din("conv_w_in", [2, D, 4 * DI])
    conv_w_d = din("conv_w", [2, 3, DI])
    conv_w_out_d = din("conv_w_out", [2, DI, D])
    ssm_norm_d = din("ssm_norm", [2, D])
    ssm_w_in_d = din("ssm_w_in", [2, D, 2 * DI])
    a_re_d = din("ssm_a_re", [2, G, P])
    a_im_d = din("ssm_a_im", [2, G, P])
    ldt_d = din("ssm_log_dt", [2, G])
    b_re_d = din("ssm_b_re", [2, G, P, H])
    b_im_d = din("ssm_b_im", [2, G, P, H])
    c_re_d = din("ssm_c_re", [2, G, H, P])
    c_im_d = din("ssm_c_im", [2, G, H, P])
    ssm_d_d = din("ssm_d", [2, DI])
    w_glu_d = din("ssm_w_glu", [2, DI, DI])
    b_glu_d = din("ssm_b_glu", [2, DI])
    ssm_w_out_d = din("ssm_w_out", [2, DI, D])
    fnorm_d = din("final_norm", [D])

    yp_o = dout("yp", [SEQ, D])
    ys_o = dout("ys", [NS, D])
    ncp_o = dout("ncp", [2, 2, DI])
    ncs_o = dout("ncs", [2, NS, 2, DI])
    nrp_o = dout("nrp", [2, G, P])
    nip_o = dout("nip", [2, G, P])
    nrs_o = dout("nrs", [2, NS, G, P])
    nis_o = dout("nis", [2, NS, G, P])
    dbg_o = dout("dbg", [5, 128, KD, NX]) if dump_x else None
    dbgA_o = nc.dram_tensor("dbgA", [2, 128, QI, NS], BF16, kind="ExternalOutput").ap() if dump_x else None
    dbgS_o = dout("dbgS", [2, 128, 4, NS]) if dump_x else None

    with es:
        pr = Prog(nc, es)

        def sb(name, shape, dt=F32):
            return es.enter_context(nc.sbuf_tensor(name, list(shape), dt))

        def ps(name, shape, dt=F32):
            return es.enter_context(nc.psum_tensor(name, list(shape), dt))

        x_t = sb("x", [128, KD, NX])
        hT_t = sb("hT", [128, KD, NX], BF16)
        A_t = sb("A", [128, QI, NX], BF16)
        B_t = sb("B", [128, QI * NX], BF16)
        NWS = 4
        wbuf = [sb("w%d" % i, [128, KD, 128], BF16) for i in range(NWS)]
        ident = sb("ident", [128, 128])
        identb = sb("identb", [128, 128], BF16)
        onesb = sb("onesb", [128, 128], BF16)
        maskT = sb("maskT", [128, 8, 16])
        gains = sb("gains", [128, 5, KD])
        convw = sb("convw", [128, 2, 3, QI])
        dnat = sb("dnat", [128, 2, QI])
        bglu = sb("bglu", [128, 2, QI])
        dgm = sb("dgm", [128, 2, G])
        Xfin = sb("Xfin", [128, 2, 2, 128])
        ucar = sb("ucar", [128, 2, QI, 2])
        rstd = sb("rstd", [128, NX])
        scr = sb("scr", [128, 4, 516])
        scrf0 = scr[:].rearrange("p a b -> p (a b)")
        rtmp = scrf0[:, 0:NX]
        smp = sb("smp", [128, 3, 576])
        smpf = smp[:].rearrange("p a b -> p (a b)")
        xin_v = hT_t[:].rearrange("p a b -> p (a b)")[:, 0:2 * D].bitcast(F32)
        yfm_v = A_t[:].rearrange("p a b -> p (a b)")[:, 0:2 * KD * 128].bitcast(F32).rearrange("p (k n) -> p k n", k=KD)
        sq_v = B_t[:, 0:KD * NX].rearrange("p (k n) -> p k n", k=KD)
        uh_v = B_t[:, 0:G * NCH].rearrange("p (g c) -> p g c", c=NCH)
        y3_v = B_t[:].rearrange("p (q n) -> p q n", q=QI)
        uhs = sb("uhs", [128, G, NS], BF16)

        mm = [PT(ps("mm%d" % i, [128, 512]), "mm%d" % i) for i in range(4)]
        sm = [PT(ps("sm%d" % i, [128, 512]), "sm%d" % i) for i in range(2)]
        tp = [PT(ps("tp%d" % i, [128, 512]), "tp%d" % i) for i in range(2)]

        st = {"w": 0, "mm": 0, "sm": 0, "tp": 0, "TP": 0, "tpx": 0, "smx": 0}

        def nxt(kind, n):
            i = st[kind]
            st[kind] = (i + 1) % n
            return i

        pr.op("pool", lambda e: e.memset(ident[:], 1.0), writes=["ident"])
        pr.op("pool", lambda e: e.affine_select(out=ident[:], in_=ident[:], pattern=[[-1, 128]],
                                                compare_op=ALU.is_equal, fill=0.0, base=0, channel_multiplier=1),
              reads=["ident"], writes=["ident"])
        pr.op("pool", lambda e: e.memset(onesb[:], 1.0), writes=["onesb"])
        pr.op("pool", lambda e: e.memset(maskT[:], 1.0), writes=["maskT"])
        pr.op("pool", lambda e: e.affine_select(out=maskT[:], in_=maskT[:], pattern=[[16, 8], [0, 16]],
                                                compare_op=ALU.is_ge, fill=0.0, base=15, channel_multiplier=-1),
              reads=["maskT"], writes=["maskT"])
        pr.op("pool", lambda e: e.memset(ucar[:], 0.0), writes=["ucar"])
        pr.op("pool", lambda e: e.memset(Xfin[:], 0.0), writes=["Xfin"])
        pr.op("pool", lambda e: e.memset(uhs[:], 0.0), writes=["uhs"])
        pr.op("dve", lambda e: e.tensor_copy(identb[:], ident[:]), reads=["ident"], writes=["identb"])

        pstage = scrf0
        plist = []
        for i, src in enumerate([conv_norm_d[0], ssm_norm_d[0], conv_norm_d[1], ssm_norm_d[1], fnorm_d]):
            plist.append((src, gains[:, i, :], "gains"))
        for l in range(2):
            for k in range(3):
                plist.append((conv_w_d[l, k], convw[:, l, k, :], "convw"))
            plist.append((ssm_d_d[l], dnat[:, l, :], "dnat"))
            plist.append((b_glu_d[l], bglu[:, l, :], "bglu"))
        for i, (src, dst, res) in enumerate(plist):
            n = src.shape[0] // 128
            pr.dma("sp", pstage[0:n, i * 128:(i + 1) * 128], src.rearrange("(k p) -> k p", p=128), writes=["scr0", "scr1", "scr2", "scr3"], dsem="par")
        pr.seal("par", ["scr0", "scr1", "scr2", "scr3"])
        for i, (src, dst, res) in enumerate(plist):
            n = src.shape[0] // 128
            tpt = tp[nxt("tp", 2)]
            pr.op("pe", lambda e, tpt=tpt, i=i, n=n: e.transpose(tpt[:, 0:n], pstage[0:n, i * 128:(i + 1) * 128], ident[0:n, 0:n]),
                  reads=["scr0", "scr1", "scr2", "scr3", "ident"], writes=[tpt.name])
            pr.op("dve", lambda e, tpt=tpt, dst=dst, n=n: e.tensor_copy(dst, tpt[:, 0:n]), reads=[tpt.name], writes=[res])

        Esel = scr[:].rearrange("p a b -> p (a b)")[:, 0:1024].rearrange("p (a b) -> p a b", a=8)
        pr.op("pool", lambda e: e.memset(Esel, 1.0), writes=["scr0", "scr1"])
        for gl in range(8):
            pr.op("pool", lambda e, gl=gl: e.affine_select(out=Esel[:, gl, :].rearrange("p (j h) -> p j h", j=8),
                                                           in_=Esel[:, gl, :].rearrange("p (j h) -> p j h", j=8),
                                                           pattern=[[0, 8], [-1, 16]], compare_op=ALU.is_equal, fill=0.0,
                                                           base=-16 * gl, channel_multiplier=1),
                  reads=["scr0", "scr1"], writes=["scr0", "scr1"])
        for l in range(2):
            pst = sm[nxt("sm", 2)]
            pv = pst[:, 0:256].rearrange("p (q g) -> p g q", g=8)
            for gl in range(8):
                pr.op("pe", lambda e, gl=gl, pv=pv, l=l: e.matmul(pv[:, gl, :], Esel[:, gl, :], dnat[:, l, :], start=True, stop=True),
                      reads=["scr0", "scr1", "dnat"], writes=[pst.name])
            pr.op("dve", lambda e, pst=pst, l=l: e.tensor_copy(dgm[:, l, :], pst[:, 0:256]), reads=[pst.name], writes=["dgm"])

        def colgroups(seg):
            return [(0, NT)] + ([(NT, NS)] if seg == 0 else [])

        def ncols(seg):
            return NX if seg == 0 else NT

        def load_weights(view):
            KC = view.shape[0] // 128
            parts = []
            for k0 in range(0, KC, KD):
                s = nxt("w", NWS)
                pr.dma("pool", wbuf[s][:, :, :], view[k0 * 128:(k0 + KD) * 128, :].rearrange("(k p) n -> p k n", p=128),
                       writes=[("w", s)], dsem="w%d" % s)
                parts.append((s, k0))
            return parts, KC

        def mm_chunk(view, rhs_fn, rhs_res, seg, big_bank, small_tile, small_off):
            parts, KC = load_weights(view)
            outs = []
            for (c0, n) in colgroups(seg):
                if n == NT:
                    o = mm[big_bank][:, 0:NT]
                    res = mm[big_bank].name
                else:
                    o = small_tile[:, small_off:small_off + NS]
                    res = small_tile.name
                for (s, k0) in parts:
                    for kk in range(KD):
                        k = k0 + kk
                        pr.op("pe", lambda e, o=o, s=s, kk=kk, k=k, c0=c0, n=n: e.matmul(o, wbuf[s][:, kk, :], rhs_fn(k, c0, n),
                                                                                        start=(k == 0), stop=(k == KC - 1)),
                              reads=[("w", s)] + rhs_res, writes=[res], inc=(k == KC - 1))
                outs.append((o, res, c0, n))
            return outs

        def rmsnorm(gi, seg, jmajor):
            rms_stats(seg)
            for k in range(KD):
                if jmajor:
                    o = hT_t[:, k, 0:NT].rearrange("p (j c) -> p c j", j=8)
                    i0 = x_t[:, k, 0:NT].rearrange("p (c j) -> p c j", j=8)
                    i1 = rstd[:, 0:NT].rearrange("p (c j) -> p c j", j=8)
                else:
                    o, i0, i1 = hT_t[:, k, 0:NT], x_t[:, k, 0:NT], rstd[:, 0:NT]
                pr.op("dve", lambda e, o=o, i0=i0, i1=i1, k=k: e.scalar_tensor_tensor(o, i0, gains[:, gi, k:k + 1], i1, ALU.mult, ALU.mult),
                      reads=["x", "rstd", "gains"], writes=["hT"])
                if seg == 0:
                    pr.op("dve", lambda e, k=k: e.scalar_tensor_tensor(hT_t[:, k, NT:NX], x_t[:, k, NT:NX], gains[:, gi, k:k + 1],
                                                                       rstd[:, NT:NX], ALU.mult, ALU.mult),
                          reads=["x", "rstd", "gains"], writes=["hT"])

        def load_x(seg):
            for t in range(NT // 128):
                r0 = seg * NT + t * 128
                pr.dma("sp", xin_v, xp_d[r0:r0 + 128, :], writes=["hT"], dsem="xin")
                for kb in range(4):
                    tpt = tp[nxt("tp", 2)]
                    for kk in range(4):
                        k = kb * 4 + kk
                        pr.op("pe", lambda e, tpt=tpt, kk=kk, k=k: e.transpose(tpt[:, kk * 128:(kk + 1) * 128], xin_v[:, k * 128:(k + 1) * 128], ident[:]),
                              reads=["hT", "ident"], writes=[tpt.name], inc=(kk == 3))
                    pr.op("dve", lambda e, tpt=tpt, kb=kb, t=t: e.tensor_copy(x_t[:, kb * 4:kb * 4 + 4, t * 128:(t + 1) * 128],
                                                                              tpt[:, :].rearrange("p (a b) -> p a b", a=4)),
                          reads=[tpt.name], writes=["x"])
            if seg == 0:
                pr.dma("sp", xin_v[0:NS, :], xs_d[:, :], writes=["hT"], dsem="xin")
                tpt = tp[nxt("tp", 2)]
                for k in range(KD):
                    pr.op("pe", lambda e, tpt=tpt, k=k: e.transpose(tpt[:, k * NS:(k + 1) * NS], xin_v[0:NS, k * 128:(k + 1) * 128], ident[0:NS, 0:NS]),
                          reads=["hT", "ident"], writes=[tpt.name], inc=(k == KD - 1))
                pr.op("dve", lambda e, tpt=tpt: e.tensor_copy(x_t[:, :, NT:NX], tpt[:, 0:KD * NS].rearrange("p (a b) -> p a b", a=KD)),
                      reads=[tpt.name], writes=["x"])

        def final_out(seg):
            n = ncols(seg)
            rms_stats(seg)
            for t in range(NT // 128):
                for k in range(KD):
                    pr.op("dve", lambda e, k=k, t=t: e.scalar_tensor_tensor(yfm_v[:, k, :], x_t[:, k, t * 128:(t + 1) * 128], gains[:, 4, k:k + 1],
                                                                            rstd[:, t * 128:(t + 1) * 128], ALU.mult, ALU.mult),
                          reads=["x", "rstd", "gains"], writes=["A"])
                for kb in range(4):
                    tpt = tp[nxt("tp", 2)]
                    for kk in range(4):
                        k = kb * 4 + kk
                        pr.op("pe", lambda e, tpt=tpt, kk=kk, k=k: e.transpose(tpt[:, kk * 128:(kk + 1) * 128], yfm_v[:, k, :], ident[:]),
                              reads=["A", "ident"], writes=[tpt.name], inc=(kk == 3))
                    pr.op("act", lambda e, tpt=tpt, kb=kb: e.copy(xin_v[:, kb * 512:(kb + 1) * 512], tpt[:, :]),
                          reads=[tpt.name], writes=["hT"])
                r0 = seg * NT + t * 128
                pr.dma("sp", yp_o[r0:r0 + 128, :], xin_v, reads=["hT"], dsem="yx")
            if seg == 0:
                for k in range(KD):
                    pr.op("dve", lambda e, k=k: e.scalar_tensor_tensor(yfm_v[:, k, 0:NS], x_t[:, k, NT:NX], gains[:, 4, k:k + 1],
                                                                       rstd[:, NT:NX], ALU.mult, ALU.mult),
                          reads=["x", "rstd", "gains"], writes=["A"])
                for kb in range(4):
                    tpt = tp[nxt("tp", 2)]
                    for kk in range(4):
                        k = kb * 4 + kk
                        pr.op("pe", lambda e, tpt=tpt, kk=kk, k=k: e.transpose(tpt[0:NS, kk * 128:(kk + 1) * 128], yfm_v[:, k, 0:NS], ident[:]),
                              reads=["A", "ident"], writes=[tpt.name], inc=(kk == 3))
                    pr.op("act", lambda e, tpt=tpt, kb=kb: e.copy(xin_v[0:NS, kb * 512:(kb + 1) * 512], tpt[0:NS, :]),
                          reads=[tpt.name], writes=["hT"])
                pr.dma("sp", ys_o[:, :], xin_v[0:NS, :], reads=["hT"], dsem="yx")

        def rms_stats(seg):
            n = ncols(seg)
            for k in range(KD):
                pr.op("act", lambda e, k=k: e.activation(out=sq_v[:, k, 0:n], in_=x_t[:, k, 0:n], func=AF.Square),
                      reads=["x"], writes=["B"])
            for (c0, nn) in colgroups(seg):
                pst = sm[nxt("sm", 2)]
                for k in range(KD):
                    pr.op("pe", lambda e, k=k, c0=c0, nn=nn, pst=pst: e.matmul(pst[:, 0:nn], onesb[:], sq_v[:, k, c0:c0 + nn],
                                                                               start=(k == 0), stop=(k == KD - 1)),
                          reads=["onesb", "B"], writes=[pst.name], inc=(k == KD - 1))
                pr.op("act", lambda e, c0=c0, nn=nn, pst=pst: e.activation(out=rtmp[:, c0:c0 + nn], in_=pst[:, 0:nn], func=AF.Sqrt,
                                                                           bias=EPS, scale=1.0 / D),
                      reads=[pst.name], writes=["scr0", "scr1"])
                pr.op("dve", lambda e, c0=c0, nn=nn: e.reciprocal(rstd[:, c0:c0 + nn], rtmp[:, c0:c0 + nn]),
                      reads=["scr0", "scr1"], writes=["rstd"])

        def out_proj(w_d, src_v, src_res, seg, jmajor):
            for m in range(KD):
                bank = nxt("mm", 4)
                smt = sm[nxt("sm", 2)]
                outs = mm_chunk(w_d[:, m * 128:(m + 1) * 128], lambda k, c0, n: src_v[:, k, c0:c0 + n], [src_res], seg, bank, smt, 0)
                for (o, res, c0, n) in outs:
                    if n == NT and jmajor:
                        xo = x_t[:, m, 0:NT].rearrange("p (c j) -> p c j", j=8)
                        oo = o.rearrange("p (j c) -> p c j", j=8)
                    else:
                        xo, oo = x_t[:, m, c0:c0 + n], o
                    pr.op("dve", lambda e, xo=xo, oo=oo: e.tensor_tensor(xo, xo, oo, ALU.add), reads=[res, "x"], writes=["x"])

        def conv_layer(l, seg, last_seg):
            gi = 2 * l
            rmsnorm(gi, seg, jmajor=False)
            w_in = conv_w_in_d[l]
            ue = scr[:, 0, 0:NT + 2]
            ctmp = scr[:, 1, 0:NT]
            cv = scr[:, 2, 0:NT]
            sz = scr[:, 3, 0:NT]
            stc = smpf[:, 0:1024].rearrange("p (q r s) -> p q r s", q=QI, r=2)
            us = smp[:, 2, 0:QI * NS].rearrange("p (q s) -> p q s", q=QI)
            stmp = smp[:, 2, 512:576].rearrange("p (a s) -> p a s", a=4)
            if seg == 0:
                nat = scrf0[0:NS, 0:2048]
                for half in range(4):
                    r, hc = half // 2, half % 2
                    pr.dma("sp", nat, sconv_d[l, :, r, hc * 2048:(hc + 1) * 2048], writes=["scr0", "scr1", "scr2", "scr3"], dsem="smp")
                    tpt = tp[nxt("tp", 2)]
                    for qq in range(16):
                        pr.op("pe", lambda e, tpt=tpt, qq=qq: e.transpose(tpt[:, qq * NS:(qq + 1) * NS], nat[:, qq * 128:(qq + 1) * 128], ident[0:NS, 0:NS]),
                              reads=["scr0", "scr1", "scr2", "scr3", "ident"], writes=[tpt.name], inc=(qq == 15))
                    pr.op("dve", lambda e, tpt=tpt, r=r, hc=hc: e.tensor_copy(stc[:, hc * 16:(hc + 1) * 16, r, :],
                                                                              tpt[:, 0:16 * NS].rearrange("p (q s) -> p q s", q=16)),
                          reads=[tpt.name], writes=["smp0", "smp1"])
                pr.dma("sp", ncs_o[l, :, 0, :], sconv_d[l, :, 1, :], dsem="ycopy")
            for q in range(QI):
                smt = sm[nxt("sm", 2)]
                banks = {}
                outs = {}
                for pi, name in enumerate(["c", "v", "b", "z"]):
                    col = ({"b": 0, "c": 1, "v": 2, "z": 3}[name]) * DI + q * 128
                    bank = nxt("mm", 4)
                    outs[name] = mm_chunk(w_in[:, col:col + 128], lambda k, c0, n: hT_t[:, k, c0:c0 + n], ["hT"], seg, bank, smt, pi * NS)
                (oc, rc, _, _), (ov, rv, _, _) = outs["c"][0], outs["v"][0]
                (ob, rb, _, _), (oz, rz, _, _) = outs["b"][0], outs["z"][0]
                pr.op("act", lambda e, q=q: e.copy(ue[:, 0:2], ucar[:, l, q, :]), reads=["ucar"], writes=["scr0"])
                pr.op("act", lambda e, oc=oc: e.copy(ctmp, oc), reads=[rc], writes=["scr1"])
                pr.op("dve", lambda e, ov=ov: e.tensor_tensor(ue[:, 2:NT + 2], ctmp, ov, ALU.mult), reads=["scr1", rv], writes=["scr0"])
                pr.op("act", lambda e, q=q: e.copy(ucar[:, l, q, :], ue[:, NT:NT + 2]), reads=["scr0"], writes=["ucar"])
                pr.op("dve", lambda e, q=q: e.tensor_scalar(cv, ue[:, 0:NT], convw[:, l, 0, q:q + 1], None, ALU.mult), reads=["scr0", "convw"], writes=["scr2"])
                pr.op("dve", lambda e, q=q: e.scalar_tensor_tensor(cv, ue[:, 1:NT + 1], convw[:, l, 1, q:q + 1], cv, ALU.mult, ALU.add),
                      reads=["scr0", "scr2", "convw"], writes=["scr2"])
                pr.op("dve", lambda e, q=q: e.scalar_tensor_tensor(cv, ue[:, 2:NT + 2], convw[:, l, 2, q:q + 1], cv, ALU.mult, ALU.add),
                      reads=["scr0", "scr2", "convw"], writes=["scr2"])
                pr.op("act", lambda e, oz=oz: e.activation(out=sz, in_=oz, func=AF.Silu), reads=[rz], writes=["scr3"])
                pr.op("dve", lambda e, ob=ob: e.tensor_tensor(sz, sz, ob, ALU.mult), reads=["scr3", rb], writes=["scr3"])
                pr.op("dve", lambda e, q=q: e.tensor_tensor(A_t[:, q, 0:NT], sz, cv, ALU.mult), reads=["scr3", "scr2"], writes=["A"])
                if seg == 0:
                    sres = [smt.name] * 4
                    pr.op("act", lambda e, smt=smt: e.copy(stmp[:, 0, :], smt[:, 0:NS]), reads=[sres[0]], writes=["smp3"])
                    pr.op("dve", lambda e, smt=smt, q=q: e.tensor_tensor(us[:, q, :], stmp[:, 0, :], smt[:, NS:2 * NS], ALU.mult),
                          reads=["smp3", sres[1]], writes=["smp2"])
                    pr.op("dve", lambda e, q=q: e.tensor_scalar(stmp[:, 1, :], stc[:, q, 0, :], convw[:, l, 0, q:q + 1], None, ALU.mult),
                          reads=["smp0", "smp1", "convw"], writes=["smp3"])
                    pr.op("dve", lambda e, q=q: e.scalar_tensor_tensor(stmp[:, 1, :], stc[:, q, 1, :], convw[:, l, 1, q:q + 1], stmp[:, 1, :], ALU.mult, ALU.add),
                          reads=["smp0", "smp1", "smp3", "convw"], writes=["smp3"])
                    pr.op("dve", lambda e, q=q: e.scalar_tensor_tensor(stmp[:, 1, :], us[:, q, :], convw[:, l, 2, q:q + 1], stmp[:, 1, :], ALU.mult, ALU.add),
                          reads=["smp2", "smp3", "convw"], writes=["smp3"])
                    pr.op("act", lambda e, smt=smt: e.activation(out=stmp[:, 2, :], in_=smt[:, 3 * NS:4 * NS], func=AF.Silu), reads=[sres[3]], writes=["smp3"])
                    pr.op("dve", lambda e, smt=smt: e.tensor_tensor(stmp[:, 2, :], stmp[:, 2, :], smt[:, 2 * NS:3 * NS], ALU.mult),
                          reads=["smp3", sres[2]], writes=["smp3"])
                    pr.op("dve", lambda e, q=q: e.tensor_tensor(A_t[:, q, NT:NX], stmp[:, 2, :], stmp[:, 1, :], ALU.mult), reads=["smp3"], writes=["A"])
            if seg == 0:
                stage = smp[0:NS, 0, :]
                for part in range(8):
                    ti = nxt("tp", 2)
                    tpt = tp[ti]
                    for qq in range(4):
                        q = part * 4 + qq
                        pr.op("pe", lambda e, tpt=tpt, qq=qq, q=q: e.transpose(tpt[0:NS, qq * 128:(qq + 1) * 128], us[:, q, :], ident[:]),
                              reads=["smp2", "ident"], writes=["tp%d" % ti], inc=(qq == 3))
                    pr.op("act", lambda e, tpt=tpt: e.copy(stage[:, 0:512], tpt[0:NS, 0:512]), reads=["tp%d" % ti], writes=["smp0"])
                    pr.dma("sp", ncs_o[l, :, 1, part * 512:(part + 1) * 512], stage[:, 0:512], reads=["smp0"], dsem="ysmp")
            if dump_x and seg == 0:
                pr.dma("sp", dbgA_o[l], A_t[:, :, NT:NX], reads=["A"], dsem="dbg")
                pr.dma("sp", dbgS_o[l], stmp, reads=["smp3"], dsem="dbg")
            out_proj(conv_w_out_d[l], A_t, "A", seg, jmajor=False)
            if last_seg:
                stage2 = smp[0:2, 0, :]
                for part in range(8):
                    tpt = tp[nxt("tp", 2)]
                    for qq in range(4):
                        q = part * 4 + qq
                        pr.op("pe", lambda e, tpt=tpt, qq=qq, q=q: e.transpose(tpt[0:2, qq * 128:(qq + 1) * 128], ucar[:, l, q, :], ident[:]),
                              reads=["ucar", "ident"], writes=[tpt.name], inc=(qq == 3))
                    pr.op("act", lambda e, tpt=tpt: e.copy(stage2[:, 0:512], tpt[0:2, 0:512]), reads=[tpt.name], writes=["smp0"])
                    pr.dma("sp", ncp_o[l, :, part * 512:(part + 1) * 512], stage2[:, 0:512], reads=["smp0"], dsem="ysmp")

        from types import SimpleNamespace
        ctx = SimpleNamespace(**{k: v for k, v in locals().items() if not k.startswith("__")})
        ssm_layer = make_ssm_layer(ctx)
        for seg in range(nseg):
            load_x(seg)
            li = 0
            for layer in range(nlayers):
                if layer % 2 == 0:
                    conv_layer(layer // 2, seg, seg == nseg - 1)
                elif not skip_ssm:
                    ssm_layer(layer // 2, seg, seg == nseg - 1)
                if dump_x and seg == 0:
                    pr.dma("sp", dbg_o[layer], x_t[:], reads=["x"], dsem="dbg")
            final_out(seg)
        pr.final_wait("sp")
        ok, stuck, _ = pr.check_deadlock()
        if not ok:
            raise RuntimeError("sync deadlock detected offline: %r" % (stuck,))
        pr.emit()
    return nc


def make_ssm_layer(c):
    pr, nc, sb, nxt = c.pr, c.nc, c.sb, c.nxt
    tp, sm, mm = c.tp, c.sm, c.mm
    ident, identb, maskT = c.ident, c.identb, c.maskT
    x_t, hT_t, A_t, B_t, uh_v, uhs, y3_v = c.x_t, c.hT_t, c.A_t, c.B_t, c.uh_v, c.uhs, c.y3_v
    scr, smp, Xfin, dgm, bglu = c.scr, c.smp, c.Xfin, c.dgm, c.bglu
    scrf = scr[:].rearrange("p a b -> p (a b)")
    SCR = ["scr0", "scr1", "scr2", "scr3"]

    aT = sb("aT", [128, 2, 128])
    NTAB = 26
    tabs = sb("tabs", [128, NTAB, 128])
    WV = sb("WV", [128, 4, 128, 8], BF16)
    BC = sb("BC", [128, 4, 32, 16], BF16)
    PQ = sb("PQ", [128, 2, 4, 8, 128], BF16)
    TP = sb("TP", [128, 3, 2, 128], BF16)
    tpx = [tp[0], tp[1], mm[2], mm[3]]
    smx = [sm[0], sm[1], mm[0], mm[1]]
    R_all = A_t[:].rearrange("p a b -> p (a b)")[:, 0:2 * 128 * NCH].rearrange("p (r g c) -> p r g c", r=2, g=128)
    hTf = hT_t[:].rearrange("p a b -> p (a b)")
    Rs_all = hTf[:, 0:4096].rearrange("p (r g s) -> p r g s", r=2, g=128)
    xsp_all = hTf[:, 4096:8192].rearrange("p (r g s) -> p r g s", r=2, g=128)

    T_ = lambda i: tabs[:, i, :]
    U_ = T_
    (I_DT, I_PHI, I_RA, I_RHO, I_RHOI, I_S, I_C, I_T1, I_T2, I_LR, I_LI, I_NR, I_NI, I_KR,
     J_KI, J_M1R, J_M1I, J_L7R, J_L7I, J_X1, J_X2, J_X3, J_X4, J_X5, I_A8R, I_A8I) = range(26)

    def dve(fn, reads, writes):
        pr.op("dve", fn, reads=reads, writes=writes)

    def act(fn, reads, writes):
        pr.op("act", fn, reads=reads, writes=writes)

    def cmul(o_r, o_i, a_r, a_i, b_r, b_i, t1, t2, res_r, res_w, tres):
        dve(lambda e: e.tensor_tensor(t1, a_r, b_r, ALU.mult), res_r, tres)
        dve(lambda e: e.tensor_tensor(t2, a_i, b_i, ALU.mult), res_r, tres)
        dve(lambda e: e.tensor_tensor(o_r, t1, t2, ALU.subtract), tres, res_w)
        dve(lambda e: e.tensor_tensor(t1, a_r, b_i, ALU.mult), res_r, tres)
        dve(lambda e: e.tensor_tensor(t2, a_i, b_r, ALU.mult), res_r, tres)
        dve(lambda e: e.tensor_tensor(o_i, t1, t2, ALU.add), tres, res_w)

    def load_params(l, seg):
        nat = scrf[:, 0:256].rearrange("p (t x) -> p t x", t=2)
        for t, src in enumerate([c.a_re_d, c.a_im_d]):
            pr.dma("sp", nat[:, t, :].rearrange("p (gh q) -> p gh q", gh=2), src[l].rearrange("(gh gp) p -> gp gh p", gh=2),
                   writes=SCR, dsem="spar")
        pr.seal("spar", SCR)
        tpt = tp[nxt("tp", 2)]
        for t in range(2):
            pr.op("pe", lambda e, t=t, tpt=tpt: e.transpose(tpt[:, t * 128:(t + 1) * 128], nat[:, t, :], ident[:]),
                  reads=SCR + ["ident"], writes=[tpt.name], inc=(t == 1))
        dve(lambda e, tpt=tpt: e.tensor_copy(aT[:].rearrange("p a b -> p (a b)"), tpt[:, 0:256]), [tpt.name], ["aT"])
        if SSM_STOP < 1.1:
            return
        ldrow = scrf[0:1, 512:768]
        pr.dma("sp", ldrow, c.ldt_d[l:l + 1, :], writes=["scr1"], dsem="spar")
        pr.seal("spar", ["scr1"])
        pst = sm[nxt("sm", 2)]
        pr.op("pe", lambda e, pst=pst: e.matmul(pst[:, 0:256], maskT[0:1, :, :].rearrange("p a b -> p (a b)"), ldrow, start=True, stop=True),
              reads=["scr1", "maskT"], writes=[pst.name])
        for gh in range(2):
            dve(lambda e, pst=pst, gh=gh: e.tensor_copy(tabs[64 * gh:64 * gh + 64, I_DT, :], pst[64 * gh:64 * gh + 64, gh * 128:(gh + 1) * 128]),
                [pst.name], ["tabs"])
        TB, TB2 = ["tabs"], ["tabs"]
        if SSM_STOP < 1.2:
            return
        aTr, aTi = aT[:, 0, :], aT[:, 1, :]
        act(lambda e: e.activation(out=T_(I_DT), in_=T_(I_DT), func=AF.Exp), TB, TB)
        dve(lambda e: e.tensor_tensor(T_(I_PHI), T_(I_DT), aTi, ALU.mult), TB + ["aT"], TB)
        dve(lambda e: e.tensor_tensor(T_(I_RA), T_(I_DT), aTr, ALU.mult), TB + ["aT"], TB)
        TWO_PI = 6.283185307179586
        C1 = 6.28125
        C2 = TWO_PI - C1
        MAGIC = 12582912.0
        X1, X2, X3 = U_(J_X1), U_(J_X2), U_(J_X3)

        def trig(j):
            dve(lambda e: e.tensor_scalar(X1, T_(I_PHI), float(j), None, ALU.mult), TB, TB)
            dve(lambda e: e.tensor_scalar(X2, X1, 1.0 / TWO_PI, MAGIC, ALU.mult, ALU.add), TB, TB)
            dve(lambda e: e.tensor_scalar(X2, X2, -MAGIC, None, ALU.add), TB, TB)
            dve(lambda e: e.scalar_tensor_tensor(X3, X2, -C1, X1, ALU.mult, ALU.add), TB, TB)
            dve(lambda e: e.scalar_tensor_tensor(X3, X2, -C2, X3, ALU.mult, ALU.add), TB, TB)
            act(lambda e: e.activation(out=T_(I_S), in_=X3, func=AF.Sin), TB, TB)
            dve(lambda e: e.tensor_scalar(X2, X1, 1.0 / TWO_PI, 0.25, ALU.mult, ALU.add), TB, TB)
            dve(lambda e: e.tensor_scalar(X2, X2, MAGIC, None, ALU.add), TB, TB)
            dve(lambda e: e.tensor_scalar(X2, X2, -MAGIC, None, ALU.add), TB, TB)
            dve(lambda e: e.scalar_tensor_tensor(X3, X2, -C1, X1, ALU.mult, ALU.add), TB, TB)
            dve(lambda e: e.scalar_tensor_tensor(X3, X2, -C2, X3, ALU.mult, ALU.add), TB, TB)
            act(lambda e: e.activation(out=T_(I_C), in_=X3, func=AF.Sin, bias=1.5707963267948966, scale=1.0), TB, TB)

        t1, t2 = U_(J_X4), U_(J_X5)
        for j in range(1, 9):
            trig(j)
            act(lambda e, j=j: e.activation(out=T_(I_RHO), in_=T_(I_RA), func=AF.Exp, scale=float(j)), TB, TB)
            act(lambda e, j=j: e.activation(out=T_(I_RHOI), in_=T_(I_RA), func=AF.Exp, scale=-float(j)), TB, TB)
            vr, vi = (T_(I_LR), T_(I_LI)) if j == 1 else ((T_(I_A8R), T_(I_A8I)) if j == 8 else (T_(I_T1), T_(I_T2)))
            dve(lambda e, vr=vr: e.tensor_tensor(vr, T_(I_RHO), T_(I_C), ALU.mult), TB, TB)
            dve(lambda e, vi=vi: e.tensor_tensor(vi, T_(I_RHO), T_(I_S), ALU.mult), TB, TB)
            dve(lambda e, j=j, vr=vr: e.tensor_copy(WV[:, 2, :, j - 1], vr), TB, ["WV"])
            dve(lambda e, j=j, vi=vi: e.tensor_copy(WV[:, 3, :, j - 1], vi), TB, ["WV"])
            mr, mi = (U_(J_L7R), U_(J_L7I)) if j == 7 else (U_(J_M1R), U_(J_M1I))
            dve(lambda e, mr=mr: e.tensor_tensor(mr, T_(I_RHOI), T_(I_C), ALU.mult), TB, TB)
            dve(lambda e, mi=mi: e.scalar_tensor_tensor(mi, T_(I_RHOI), -1.0, T_(I_S), ALU.mult, ALU.mult), TB, TB)
            if j == 1:
                dve(lambda e: e.tensor_scalar(X1, T_(I_LR), -1.0, None, ALU.add), TB, TB)
                dve(lambda e: e.tensor_tensor(X2, X1, aTr, ALU.mult), TB + ["aT"], TB)
                dve(lambda e: e.tensor_tensor(X3, T_(I_LI), aTi, ALU.mult), TB + ["aT"], TB)
                dve(lambda e: e.tensor_tensor(T_(I_NR), X2, X3, ALU.add), TB, TB)
                dve(lambda e: e.tensor_tensor(X2, T_(I_LI), aTr, ALU.mult), TB + ["aT"], TB)
                dve(lambda e: e.tensor_tensor(X3, X1, aTi, ALU.mult), TB + ["aT"], TB)
                dve(lambda e: e.tensor_tensor(T_(I_NI), X2, X3, ALU.subtract), TB, TB)
                dve(lambda e: e.tensor_tensor(X2, aTr, aTr, ALU.mult), ["aT"], TB)
                dve(lambda e: e.tensor_tensor(X3, aTi, aTi, ALU.mult), ["aT"], TB)
                dve(lambda e: e.tensor_tensor(X2, X2, X3, ALU.add), TB, TB)
                dve(lambda e: e.reciprocal(X3, X2), TB, TB)
                dve(lambda e: e.tensor_tensor(T_(I_KR), T_(I_NR), X3, ALU.mult), TB, TB)
                dve(lambda e: e.tensor_tensor(U_(J_KI), T_(I_NI), X3, ALU.mult), TB, TB)
            cmul(T_(I_NR), T_(I_NI), mr, mi, T_(I_KR), U_(J_KI), t1, t2, TB, TB, TB)
            dve(lambda e, j=j: e.tensor_copy(WV[:, 0, :, j - 1], T_(I_NR)), TB, ["WV"])
            dve(lambda e, j=j: e.tensor_copy(WV[:, 1, :, j - 1], T_(I_NI)), TB, ["WV"])

    def load_bc(l, qt):
        g0 = 32 * qt
        bst = scrf[:, 0:512].rearrange("p (g h) -> p g h", h=16)
        for t, src in enumerate([c.b_re_d, c.b_im_d]):
            for gh in range(2):
                pr.dma("sp", bst[64 * gh:64 * gh + 64, :, :], src[l, gh * 128 + g0:gh * 128 + g0 + 32].rearrange("g p h -> p g h"),
                       writes=["scr0"], dsem="spar")
            pr.seal("spar", ["scr0"])
            act(lambda e, t=t: e.copy(BC[:, t, :, :], bst), ["scr0"], ["BC"])
        cst = scrf[:, 1032:1032 + 512].rearrange("p (gb x) -> p gb x", gb=4)
        for t, src in enumerate([c.c_re_d, c.c_im_d]):
            sv = src[l].rearrange("(gh gq gb gl) h p -> gq (gl h) gb gh p", gh=2, gq=4, gb=4)
            for gh in range(2):
                pr.dma("sp", cst[:, :, gh * 64:(gh + 1) * 64], sv[qt][:, :, gh, :], writes=["scr2"], dsem="spar")
            pr.seal("spar", ["scr2"])
            tpt = tp[nxt("tp", 2)]
            for k in range(4):
                pr.op("pe", lambda e, tpt=tpt, k=k: e.transpose(tpt[:, k * 128:(k + 1) * 128], cst[:, k, :], ident[:]),
                      reads=["scr2", "ident"], writes=[tpt.name], inc=(k == 3))
            act(lambda e, tpt=tpt, t=t: e.copy(BC[:, 2 + t, :, :].rearrange("p g h -> p (g h)"), tpt[:, 0:512]), [tpt.name], ["BC"])

    def gen_tables(sub, buf, want_q):
        g0 = sub * 8
        t1 = scrf[:, 0:1024].rearrange("p (g j h) -> p g j h", g=8, j=8)
        t2 = scrf[:, 1024:2048].rearrange("p (g j h) -> p g j h", g=8, j=8)
        res_pq = ("PQ", buf)

        def tab(i):
            return WV[:, i, g0:g0 + 8, :].unsqueeze(3).broadcast_to([128, 8, 8, 16])

        gq0 = (sub % 4) * 8

        def bc(i):
            return BC[:, i, gq0:gq0 + 8, :].unsqueeze(2).broadcast_to([128, 8, 8, 16])

        def out(i):
            return PQ[:, buf, i, :, :].rearrange("p g (j h) -> p g j h", j=8)
        rd = ["WV", "BC"]
        dve(lambda e: e.tensor_tensor(t1, tab(0), bc(0), ALU.mult), rd, ["scr0", "scr1"])
        dve(lambda e: e.tensor_tensor(t2, tab(1), bc(1), ALU.mult), rd, ["scr2", "scr3"])
        dve(lambda e: e.tensor_tensor(out(0), t1, t2, ALU.subtract), SCR, [res_pq])
        dve(lambda e: e.tensor_tensor(t1, tab(0), bc(1), ALU.mult), rd, ["scr0", "scr1"])
        dve(lambda e: e.tensor_tensor(t2, tab(1), bc(0), ALU.mult), rd, ["scr2", "scr3"])
        dve(lambda e: e.tensor_tensor(out(1), t1, t2, ALU.add), SCR, [res_pq])
        if want_q:
            dve(lambda e: e.tensor_tensor(t1, tab(2), bc(2), ALU.mult), rd, ["scr0", "scr1"])
            dve(lambda e: e.tensor_tensor(t2, tab(3), bc(3), ALU.mult), rd, ["scr2", "scr3"])
            dve(lambda e: e.tensor_tensor(out(2), t1, t2, ALU.subtract), SCR, [res_pq])
            dve(lambda e: e.tensor_tensor(t1, tab(2), bc(3), ALU.mult), rd, ["scr0", "scr1"])
            dve(lambda e: e.tensor_tensor(t2, tab(3), bc(2), ALU.mult), rd, ["scr2", "scr3"])
            dve(lambda e: e.tensor_tensor(t1, t1, t2, ALU.add), SCR, ["scr0", "scr1"])
            dve(lambda e: e.tensor_scalar(PQ[:, buf, 3, :, :].rearrange("p g x -> p (g x)"), scrf[:, 0:1024], -1.0, None, ALU.mult), ["scr0", "scr1"], [res_pq])

    def ssm_core(l, seg, last_seg):
        ncol_s = NS if seg == 0 else 0
        if seg == 0:
            pr.op("pool", lambda e: e.memset(uhs[0:112, :, :], 0.0), writes=["uhs"])
        Av = A_t[:, :, 0:NT].rearrange("p q (j c) -> p q j c", j=8)
        uhq = uh_v.rearrange("p (q gl) c -> p q gl c", gl=8)
        uhsq = uhs[:].rearrange("p (q gl) s -> p q gl s", gl=8)
        for gl in range(8):
            for j in range(8):
                pr.dma("sp", uhq[16 * j:16 * j + 16, :, gl, :], Av[16 * gl:16 * gl + 16, :, j, :], reads=["A"], writes=["B"], dsem="shuf")
            if seg == 0:
                pr.dma("sp", uhsq[112:128, :, gl, :], A_t[16 * gl:16 * gl + 16, :, NT:NX], reads=["A"], writes=["uhs"], dsem="shuf")
        pr.seal("shuf", ["B", "uhs"] if seg == 0 else ["B"])

        for sub in range(16):
            buf = sub % 2
            if sub % 4 == 0:
                load_bc(l, sub // 4)
            if SSM_STOP < 2.1:
                return
            gen_tables(sub, buf, want_q=False)
            if SSM_STOP < 2.2:
                return
            for gs in range(8):
                gp = sub * 8 + gs
                rt = smx[nxt("smx", 4)]
                rtv = rt[:, 0:2 * NCH].rearrange("p (r c) -> p r c", r=2)
                rts = rt[:, 256:256 + 2 * NS].rearrange("p (r s) -> p r s", r=2)
                for gh in range(2):
                    g = gh * 128 + gp
                    lo, hi = 64 * gh, 64 * gh + 64
                    tb = nxt("TP", 3)
                    pt = tpx[nxt("tpx", 4)]
                    for r in range(2):
                        pr.op("pe", lambda e, pt=pt, r=r, lo=lo, hi=hi, buf=buf, gs=gs: e.matmul(pt[:, r * 64:(r + 1) * 64], PQ[lo:hi, buf, r, gs, :], identb[lo:hi, lo:hi], start=True, stop=True),
                              reads=[("PQ", buf), "identb"], writes=[pt.name], inc=(r == 1))
                    act(lambda e, pt=pt, tb=tb: e.copy(TP[:, tb, 1, :], pt[:, 0:128]), [pt.name], [("TP", tb)])
                    for r in range(2):
                        pr.op("pe", lambda e, rtv=rtv, r=r, lo=lo, hi=hi, tb=tb, g=g: e.matmul(rtv[lo:hi, r, :], TP[:, tb, 1, r * 64:(r + 1) * 64], uh_v[:, g, :], start=True, stop=True),
                              reads=[("TP", tb), "B"], writes=[rt.name], inc=(r == 1 and seg != 0))
                        if seg == 0:
                            pr.op("pe", lambda e, rts=rts, r=r, lo=lo, hi=hi, tb=tb, g=g: e.matmul(rts[lo:hi, r, :], TP[:, tb, 1, r * 64:(r + 1) * 64], uhs[:, g, :], start=True, stop=True),
                                  reads=[("TP", tb), "uhs"], writes=[rt.name], inc=(r == 1))
                dve(lambda e, rtv=rtv, gp=gp: e.tensor_copy(R_all[:, :, gp, :], rtv), [rt.name], ["A"])
                if seg == 0:
                    dve(lambda e, rts=rts, gp=gp: e.tensor_copy(Rs_all[:, :, gp, :], rts), [rt.name], ["hT"])
                if SSM_STOP < 2.3:
                    return

        if SSM_STOP < 3:
            return
        a8r, a8i = T_(I_A8R), T_(I_A8I)
        cur = Xfin[:, l, :, :]
        st_t = tabs[:, J_X1:J_X1 + 2, :]
        st_m = tabs[:, J_X3:J_X3 + 2, :]
        for cc in range(NCH):
            dve(lambda e, cc=cc: e.tensor_tensor(st_t, cur, R_all[:, :, :, cc], ALU.add), ["Xfin", "A"], ["tabs"])
            act(lambda e, cc=cc: e.copy(R_all[:, :, :, cc], cur), ["Xfin"], ["A"])
            dve(lambda e: e.tensor_tensor(st_m, st_t, a8r.unsqueeze(1).broadcast_to([128, 2, 128]), ALU.mult), ["tabs"], ["tabs"])
            dve(lambda e: e.tensor_tensor(st_t[:, 0, :], st_t[:, 0, :], a8i, ALU.mult), ["tabs"], ["tabs"])
            dve(lambda e: e.tensor_tensor(st_t[:, 1, :], st_t[:, 1, :], a8i, ALU.mult), ["tabs"], ["tabs"])
            dve(lambda e: e.tensor_tensor(cur[:, 0, :], st_m[:, 0, :], st_t[:, 1, :], ALU.subtract), ["tabs"], ["Xfin"])
            dve(lambda e: e.tensor_tensor(cur[:, 1, :], st_m[:, 1, :], st_t[:, 0, :], ALU.add), ["tabs"], ["Xfin"])

        if SSM_STOP < 4:
            return
        if seg == 0:
            sample_states(l)

        if SSM_STOP < 5:
            return
        for sub in range(16):
            buf = sub % 2
            if sub % 4 == 0:
                load_bc(l, sub // 4)
            gen_tables(sub, buf, want_q=True)
            for gs in range(8):
                gp = sub * 8 + gs
                for gh in range(2):
                    g = gh * 128 + gp
                    lo, hi = 64 * gh, 64 * gh + 64
                    tb = nxt("TP", 3)
                    pt = tpx[nxt("tpx", 4)]
                    for r in range(2):
                        pr.op("pe", lambda e, pt=pt, r=r, lo=lo, hi=hi, buf=buf, gs=gs: e.matmul(pt[:, 0:128], PQ[lo:hi, buf, r, gs, :], PQ[lo:hi, buf, 2 + r, gs, :], start=(r == 0), stop=(r == 1)),
                              reads=[("PQ", buf)], writes=[pt.name], inc=(r == 1))
                    dve(lambda e, pt=pt, tb=tb: e.tensor_tensor(TP[:, tb, 0, :], pt[:, 0:128], maskT[:].rearrange("p a b -> p (a b)"), ALU.mult),
                        [pt.name, "maskT"], [("TP", tb)])
                    yt = smx[nxt("smx", 4)]
                    pr.op("pe", lambda e, yt=yt, tb=tb, g=g: e.matmul(yt[:, 0:NCH], TP[:, tb, 0, :], uh_v[:, g, :], start=True, stop=False),
                          reads=[("TP", tb), "B"], writes=[yt.name], inc=False)
                    for r in range(2):
                        pr.op("pe", lambda e, yt=yt, r=r, lo=lo, hi=hi, buf=buf, gs=gs, gp=gp: e.matmul(yt[:, 0:NCH], PQ[lo:hi, buf, 2 + r, gs, :], R_all[lo:hi, r, gp, :], start=False, stop=(r == 1)),
                              reads=[("PQ", buf), "A"], writes=[yt.name], inc=(r == 1))
                    dve(lambda e, yt=yt, g=g: e.scalar_tensor_tensor(uh_v[:, g, :], uh_v[:, g, :], dgm[:, l, g:g + 1], yt[:, 0:NCH], ALU.mult, ALU.add),
                        [yt.name, "B", "dgm"], ["B"])
                    if seg == 0:
                        pr.op("pe", lambda e, yt=yt, tb=tb, g=g: e.matmul(yt[:, 128:128 + NS], TP[:, tb, 0, :], uhs[:, g, :], start=True, stop=False),
                              reads=[("TP", tb), "uhs"], writes=[yt.name], inc=False)
                        for r in range(2):
                            pr.op("pe", lambda e, yt=yt, r=r, lo=lo, hi=hi, buf=buf, gs=gs, gp=gp: e.matmul(yt[:, 128:128 + NS], PQ[lo:hi, buf, 2 + r, gs, :], xsp_all[lo:hi, r, gp, :], start=False, stop=(r == 1)),
                                  reads=[("PQ", buf), "hT"], writes=[yt.name], inc=(r == 1))
                        dve(lambda e, yt=yt, g=g: e.scalar_tensor_tensor(uhs[:, g, :], uhs[:, g, :], dgm[:, l, g:g + 1], yt[:, 128:128 + NS], ALU.mult, ALU.add),
                            [yt.name, "uhs", "dgm"], ["uhs"])

        if SSM_STOP < 6:
            return
        for gl in range(8):
            for j in range(8):
                pr.dma("sp", Av[16 * gl:16 * gl + 16, :, j, :], uhq[16 * j:16 * j + 16, :, gl, :], reads=["B"], writes=["A"], dsem="shuf")
            if seg == 0:
                pr.dma("sp", A_t[16 * gl:16 * gl + 16, :, NT:NX], uhsq[112:128, :, gl, :], reads=["uhs"], writes=["A"], dsem="shuf")
        pr.seal("shuf", ["A"])

        if last_seg:
            stg = smp[:, 0, 0:256].rearrange("p (r x) -> p r x", r=2)
            tpt = tp[nxt("tp", 2)]
            for r in range(2):
                pr.op("pe", lambda e, tpt=tpt, r=r: e.transpose(tpt[:, r * 128:(r + 1) * 128], Xfin[:, l, r, :], ident[:]),
                      reads=["Xfin", "ident"], writes=[tpt.name], inc=(r == 1))
            dve(lambda e, tpt=tpt: e.tensor_copy(stg, tpt[:, 0:256].rearrange("p (r x) -> p r x", r=2)), [tpt.name], ["smp0"])
            for r, dst in enumerate([c.nrp_o, c.nip_o]):
                pr.dma("sp", dst[l].rearrange("(gh gp) p -> gp gh p", gh=2), stg[:, r, :].rearrange("p (gh q) -> p gh q", gh=2), reads=["smp0"], dsem="ysmp")

    def sample_states(l):
        lam_r, lam_i = T_(I_LR), T_(I_LI)
        a8r, a8i = T_(I_A8R), T_(I_A8I)
        l7r, l7i = U_(J_L7R), U_(J_L7I)
        nat = smp[:, 0, 0:512].rearrange("p (sh r x) -> p sh r x", sh=2, r=2)
        x0q = smp[:, 1, 0:512].rearrange("p (r sh x) -> p r sh x", r=2, sh=2)
        xnq = smp[:, 2, 0:512].rearrange("p (r sh x) -> p r sh x", r=2, sh=2)
        t1 = scrf[:, 0:256].rearrange("p (sh sl gl) -> p sh sl gl", sh=2, sl=8)
        t2 = scrf[:, 256:512].rearrange("p (sh sl gl) -> p sh sl gl", sh=2, sl=8)
        for et in range(8):
            g0 = 16 * et
            for r, src in enumerate([c.sre_d, c.sim_d]):
                sv = src[l].rearrange("(sh sl) (gh ge gl) p -> sl ge gl sh gh p", sl=8, gh=2, ge=8)
                for sl in range(8):
                    for sh in range(2):
                        pr.dma("sp", nat[16 * sl:16 * sl + 16, sh, r, :].rearrange("p (gh q) -> p gh q", gh=2), sv[sl, et][:, sh],
                               writes=["smp0"], dsem="sst")
            pr.seal("sst", ["smp0"])
            for r in range(2):
                tpt = tp[nxt("tp", 2)]
                for sh in range(2):
                    pr.op("pe", lambda e, tpt=tpt, sh=sh, r=r: e.transpose(tpt[:, sh * 128:(sh + 1) * 128], nat[:, sh, r, :], ident[:]),
                          reads=["smp0", "ident"], writes=[tpt.name], inc=(sh == 1))
                dve(lambda e, tpt=tpt, r=r: e.tensor_copy(x0q[:, r, :, :], tpt[:, 0:256].rearrange("p (sh x) -> p sh x", sh=2)), [tpt.name], ["smp1"])

            def xv(r):
                return x0q[:, r, :, :].rearrange("p sh (sl gl) -> p sh sl gl", sl=8)

            def cf(tbl):
                return tbl[:, g0:g0 + 16].unsqueeze(1).unsqueeze(1).broadcast_to([128, 2, 8, 16])

            def spo(r):
                return xsp_all[:, r, g0:g0 + 16, :].rearrange("p gl (sh sl) -> p sh sl gl", sh=2)

            def rsv(r):
                return Rs_all[:, r, g0:g0 + 16, :].rearrange("p gl (sh sl) -> p sh sl gl", sh=2)

            def xno(r):
                return xnq[:, r, :, :].rearrange("p sh (sl gl) -> p sh sl gl", sl=8)
            RD = ["smp1", "tabs", "hT"]
            S01 = ["scr0"]

            def TT(o, a, b, op, rd, wr):
                dve(lambda e, o=o, a=a, b=b, op=op: e.tensor_tensor(o, a, b, op), rd, wr)
            TT(t1, xv(0), cf(l7r), ALU.mult, RD, S01)
            TT(t2, xv(1), cf(l7i), ALU.mult, RD, S01)
            TT(spo(0), t1, t2, ALU.subtract, S01, ["hT"])
            TT(t1, xv(0), cf(l7i), ALU.mult, RD, S01)
            TT(t2, xv(1), cf(l7r), ALU.mult, RD, S01)
            TT(spo(1), t1, t2, ALU.add, S01, ["hT"])
            TT(t1, xv(0), cf(lam_r), ALU.mult, RD, S01)
            TT(t2, xv(1), cf(lam_i), ALU.mult, RD, S01)
            TT(t1, t1, t2, ALU.subtract, S01, S01)
            TT(t2, rsv(0), cf(a8r), ALU.mult, RD, S01)
            TT(t1, t1, t2, ALU.add, S01, S01)
            TT(t2, rsv(1), cf(a8i), ALU.mult, RD, S01)
            TT(xno(0), t1, t2, ALU.subtract, S01, ["smp2"])
            TT(t1, xv(0), cf(lam_i), ALU.mult, RD, S01)
            TT(t2, xv(1), cf(lam_r), ALU.mult, RD, S01)
            TT(t1, t1, t2, ALU.add, S01, S01)
            TT(t2, rsv(0), cf(a8i), ALU.mult, RD, S01)
            TT(t1, t1, t2, ALU.add, S01, S01)
            TT(t2, rsv(1), cf(a8r), ALU.mult, RD, S01)
            TT(xno(1), t1, t2, ALU.add, S01, ["smp2"])
            for r, dst in enumerate([c.nrs_o, c.nis_o]):
                tpt = tp[nxt("tp", 2)]
                for sh in range(2):
                    pr.op("pe", lambda e, tpt=tpt, sh=sh, r=r: e.transpose(tpt[:, sh * 128:(sh + 1) * 128], xnq[:, r, sh, :], ident[:]),
                          reads=["smp2", "ident"], writes=[tpt.name], inc=(sh == 1))
                dve(lambda e, tpt=tpt, r=r: e.tensor_copy(nat[:, :, r, :], tpt[:, 0:256].rearrange("p (sh x) -> p sh x", sh=2)), [tpt.name], ["smp0"])
                dv = dst[l].rearrange("(sh sl) (gh ge gl) p -> sl ge gl sh gh p", sl=8, gh=2, ge=8)
                for sl in range(8):
                    for sh in range(2):
                        pr.dma("sp", dv[sl, et][:, sh], nat[16 * sl:16 * sl + 16, sh, r, :].rearrange("p (gh q) -> p gh q", gh=2),
                               reads=["smp0"], dsem="sst")
            pr.seal("sst", [])

    def ssm_layer(l, seg, last_seg):
        gi = 2 * l + 1
        n = NX if seg == 0 else NT
        c.rmsnorm(gi, seg, jmajor=True)
        load_params(l, seg)
        w_in = c.ssm_w_in_d[l]
        for q in range(QI):
            bank = nxt("mm", 4)
            smt = sm[nxt("sm", 2)]
            outs = c.mm_chunk(w_in[:, q * 128:(q + 1) * 128], lambda k, c0, nn: hT_t[:, k, c0:c0 + nn], ["hT"], seg, bank, smt, 0)
            for (o, res, c0, nn) in outs:
                act(lambda e, o=o, q=q, c0=c0, nn=nn: e.copy(A_t[:, q, c0:c0 + nn], o), [res], ["A"])
        ssm_core(l, seg, last_seg)
        for q in range(QI):
            yv = A_t[:, q, 0:n]
            g1 = scrf[:, 0:n]
            g2 = scrf[:, 1032:1032 + n]
            dve(lambda e, yv=yv, g1=g1: e.tensor_tensor(g1, yv, yv, ALU.mult), ["A"], ["scr0", "scr1"])
            dve(lambda e, g1=g1: e.tensor_scalar(g1, g1, 0.044715, 1.0, ALU.mult, ALU.add), ["scr0", "scr1"], ["scr0", "scr1"])
            dve(lambda e, yv=yv, g1=g1, g2=g2: e.tensor_tensor(g2, g1, yv, ALU.mult), ["scr0", "scr1", "A"], ["scr2", "scr3"])
            act(lambda e, g2=g2: e.activation(out=g2, in_=g2, func=AF.Sigmoid, scale=1.5957691216057308), ["scr2", "scr3"], ["scr2", "scr3"])
            dve(lambda e, yv=yv, g2=g2: e.tensor_tensor(yv, yv, g2, ALU.mult), ["scr2", "scr3", "A"], ["A"])
        if seg == 0:
            c.rmsnorm(gi, seg, jmajor=True)
        for m in range(QI):
            bank_g = nxt("mm", 4)
            smt = sm[nxt("sm", 2)]
            outs_g = c.mm_chunk(c.w_glu_d[l][:, m * 128:(m + 1) * 128], lambda k, c0, nn: A_t[:, k, c0:c0 + nn], ["A"], seg, bank_g, smt, 0)
            bank_z = nxt("mm", 4)
            outs_z = c.mm_chunk(w_in[:, DI + m * 128:DI + (m + 1) * 128], lambda k, c0, nn: hT_t[:, k, c0:c0 + nn], ["hT"], seg, bank_z, smt, 64)
            for (og, rg, c0, nn), (oz, rz, _, _) in zip(outs_g, outs_z):
                s1 = scrf[:, 0:nn]
                s2 = scrf[:, 1032:1032 + nn]
                act(lambda e, og=og, s1=s1, m=m: e.activation(out=s1, in_=og, func=AF.Sigmoid, bias=bglu[:, l, m:m + 1], scale=1.0), [rg, "bglu"], ["scr0", "scr1"])
                act(lambda e, oz=oz, s2=s2: e.activation(out=s2, in_=oz, func=AF.Silu), [rz], ["scr2", "scr3"])
                dve(lambda e, s1=s1, m=m, c0=c0, nn=nn: e.tensor_tensor(s1, s1, A_t[:, m, c0:c0 + nn], ALU.mult), ["scr0", "scr1", "A"], ["scr0", "scr1"])
                dve(lambda e, s1=s1, s2=s2, m=m, c0=c0, nn=nn: e.tensor_tensor(y3_v[:, m, c0:c0 + nn], s1, s2, ALU.mult), SCR, ["B"])
        c.out_proj(c.ssm_w_out_d[l], y3_v, "B", seg, jmajor=True)

    return ssm_layer


_CACHE = {}


def kernel(**inputs):
    inputs = {k: np.ascontiguousarray(np.asarray(v, dtype=np.float32)) for k, v in inputs.items()}
    if "nc" not in _CACHE:
        _CACHE["nc"] = build_program()
    nc = _CACHE["nc"]
    wnames = ["conv_norm", "conv_w_in", "conv_w", "conv_w_out", "ssm_norm", "ssm_w_in", "ssm_a_re", "ssm_a_im",
              "ssm_log_dt", "ssm_b_re", "ssm_b_im", "ssm_c_re", "ssm_c_im", "ssm_d", "ssm_w_glu", "ssm_b_glu",
              "ssm_w_out", "final_norm"]
    in_maps = []
    for i in range(NCORES):
        b = i % 4
        m = {w: inputs[w] for w in wnames}
        m["xp"] = inputs["x_prompt"][b]
        m["xs"] = np.ascontiguousarray(inputs["x_sample"][i * NS:(i + 1) * NS, 0, :])
        m["sconv"] = np.ascontiguousarray(inputs["state_conv"][:, i * NS:(i + 1) * NS])
        m["sre"] = np.ascontiguousarray(inputs["state_ssm_re"][:, i * NS:(i + 1) * NS])
        m["sim"] = np.ascontiguousarray(inputs["state_ssm_im"][:, i * NS:(i + 1) * NS])
        in_maps.append(m)
    res = run_bass_kernel_spmd(nc, in_maps, core_ids=list(range(NCORES)))
    R = res.results
    y_prompt = np.stack([R[b]["yp"] for b in range(4)], axis=0)
    y_sample = np.concatenate([R[i]["ys"] for i in range(NCORES)], axis=0)[:, None, :]
    ncp = np.stack([R[b]["ncp"] for b in range(4)], axis=1)
    ncs = np.concatenate([R[i]["ncs"] for i in range(NCORES)], axis=1)
    nrp = np.stack([R[b]["nrp"] for b in range(4)], axis=1)
    nip = np.stack([R[b]["nip"] for b in range(4)], axis=1)
    nrs = np.concatenate([R[i]["nrs"] for i in range(NCORES)], axis=1)
    nis = np.concatenate([R[i]["nis"] for i in range(NCORES)], axis=1)
    return (y_prompt.astype(np.float32), y_sample.astype(np.float32), ncp.astype(np.float32), ncs.astype(np.float32),
            nrp.astype(np.float32), nip.astype(np.float32), nrs.astype(np.float32), nis.astype(np.float32))
```
